# Optimizing a Trainium2 kernel written in Bass

```python
import math
import jax, jax.numpy as jnp
from jax import lax
import numpy as np

D_MODEL = 1024
BATCH = 8
SEQ = 2048
DEPTH = 1

N_META = 16
D_MIX = D_MODEL
EPS = 1e-6
DIFF_HEADS = 4
DIFF_QK_DIM = 64
DIFF_V_DIM = 2 * DIFF_QK_DIM
DIFF_WIDTH = DIFF_HEADS * DIFF_V_DIM
ATTN_BLOCK = 128
REL_BUCKETS = 32
REL_MAX_DIST = 128
HGRN_HEADS = 4
HGRN_EXPAND = 128
HGRN_WIDTH = D_MIX - DIFF_WIDTH
HGRN_HEAD_V = HGRN_WIDTH // HGRN_HEADS
HGRN_KDIM = HGRN_HEADS * HGRN_EXPAND
HGRN_CHUNK = 16
COL_SIZES = [DIFF_HEADS * 2 * DIFF_QK_DIM,
             DIFF_HEADS * 2 * DIFF_QK_DIM,
             DIFF_WIDTH,
             HGRN_KDIM,
             HGRN_KDIM,
             HGRN_WIDTH,
             HGRN_WIDTH]
IN_COLS = sum(COL_SIZES)
SPLITS = [int(s) for s in np.cumsum(COL_SIZES)[:-1]]
D_FF = 2816
CONV_WIDTH = 3

kernel_name = "hymba_diffattn_hgrn2_convffn_layer"


def rmsnorm(x, g):
    xf = x.astype(jnp.float32)
    y = xf * lax.rsqrt(jnp.mean(xf * xf, axis=-1, keepdims=True) + EPS)
    return (y * g.astype(jnp.float32)).astype(x.dtype)


def rel_bucket(dist):
    n = jnp.maximum(dist, 0)
    max_exact = REL_BUCKETS // 2
    nf = jnp.maximum(n, 1).astype(jnp.float32)
    large = max_exact + (jnp.log(nf / max_exact) / math.log(REL_MAX_DIST / max_exact)
                         * (REL_BUCKETS - max_exact)).astype(jnp.int32)
    large = jnp.minimum(large, REL_BUCKETS - 1)
    return jnp.where(n < max_exact, n, large)


def diff_attention(q, k, v, lam_vecs, subln, rel_table, lam_init):
    B, L, _ = q.shape
    nblk = -(-L // ATTN_BLOCK)
    Lp = nblk * ATTN_BLOCK
    pad = Lp - L
    f32 = jnp.float32
    lv = lam_vecs.astype(f32)
    lam = jnp.exp(jnp.sum(lv[0] * lv[1])) - jnp.exp(jnp.sum(lv[2] * lv[3])) + lam_init
    qh = q.reshape(B, L, DIFF_HEADS, 2, DIFF_QK_DIM).transpose(0, 2, 3, 1, 4)
    kh = k.reshape(B, L, DIFF_HEADS, 2, DIFF_QK_DIM).transpose(0, 2, 3, 1, 4)
    vh = v.reshape(B, L, DIFF_HEADS, DIFF_V_DIM).transpose(0, 2, 1, 3)
    qh = jnp.pad(qh, ((0, 0), (0, 0), (0, 0), (0, pad), (0, 0)))
    kh = jnp.pad(kh, ((0, 0), (0, 0), (0, 0), (0, pad), (0, 0)))
    vh = jnp.pad(vh, ((0, 0), (0, 0), (0, pad), (0, 0))).astype(f32)
    qb = jnp.moveaxis(qh.reshape(B, DIFF_HEADS, 2, nblk, ATTN_BLOCK, DIFF_QK_DIM), 3, 0)
    k_pos = jnp.arange(Lp)
    scale = DIFF_QK_DIM ** -0.5

    def block(args):
        i, qblk = args
        q_pos = i * ATTN_BLOCK + jnp.arange(ATTN_BLOCK)
        dist = q_pos[:, None] - k_pos[None, :]
        bias = jnp.moveaxis(rel_table.astype(f32)[rel_bucket(dist)], -1, 0)
        s = jnp.einsum('bhmqd,bhmkd->bhmqk', qblk, kh).astype(f32) * scale + bias[None, :, None]
        s = jnp.where(dist >= 0, s, -jnp.inf)
        p = jax.nn.softmax(s, axis=-1)
        attn = p[:, :, 0] - lam * p[:, :, 1]
        return jnp.einsum('bhqk,bhkd->bhqd', attn, vh)

    o = lax.map(block, (jnp.arange(nblk), qb))
    o = jnp.moveaxis(o, 0, 2).reshape(B, DIFF_HEADS, Lp, DIFF_V_DIM)[:, :, :L]
    o = rmsnorm(o, subln) * (1.0 - lam_init)
    return o.transpose(0, 2, 1, 3).reshape(B, L, DIFF_WIDTH).astype(v.dtype)


def hgrn2(q, f_logit, i_in, g, lb, gnorm):
    B, L, _ = q.shape
    nC = L // HGRN_CHUNK
    f32 = jnp.float32
    f = lb + (1.0 - lb) * jax.nn.sigmoid(f_logit.astype(f32))
    log_f = jnp.log(f)
    kk = 1.0 - f

    def to_chunks(t, d):
        t = t.astype(f32).reshape(B, nC, HGRN_CHUNK, HGRN_HEADS, d)
        return jnp.transpose(t, (1, 0, 3, 2, 4))

    qc = to_chunks(q, HGRN_EXPAND) * (HGRN_EXPAND ** -0.5)
    kc = to_chunks(kk, HGRN_EXPAND)
    gc = to_chunks(log_f, HGRN_EXPAND)
    vc = to_chunks(i_in, HGRN_HEAD_V)
    causal = jnp.tril(jnp.ones((HGRN_CHUNK, HGRN_CHUNK), dtype=bool))

    def step(S, inp):
        qh, kh, vh, gh = inp
        G = jnp.cumsum(gh, axis=2)
        o_inter = jnp.einsum('bhtk,bhkv->bhtv', qh * jnp.exp(G), S)
        diff = G[:, :, :, None, :] - G[:, :, None, :, :]
        decay = jnp.exp(jnp.where(causal[:, :, None], diff, -jnp.inf))
        A = jnp.einsum('bhtk,bhsk,bhtsk->bhts', qh, kh, decay)
        o = o_inter + jnp.einsum('bhts,bhsv->bhtv', A, vh)
        G_last = G[:, :, -1]
        S_new = jnp.exp(G_last)[..., None] * S + jnp.einsum(
            'bhsk,bhsv->bhkv', kh * jnp.exp(G_last[:, :, None] - G), vh)
        return S_new, o

    S0 = jnp.zeros((B, HGRN_HEADS, HGRN_EXPAND, HGRN_HEAD_V), f32)
    _, o = lax.scan(step, S0, (qc, kc, vc, gc))
    o = jnp.transpose(o, (1, 0, 3, 2, 4)).reshape(B, L, HGRN_HEADS, HGRN_HEAD_V)
    gate = jax.nn.silu(g.astype(f32).reshape(B, L, HGRN_HEADS, HGRN_HEAD_V))
    o = rmsnorm(o, gnorm) * gate
    return o.reshape(B, L, HGRN_WIDTH).astype(q.dtype)


def conv_ffn(a, w_up, conv_w, conv_b, w_down):
    u = a @ w_up
    C = u.shape[-1]
    u = lax.conv_general_dilated(
        u, conv_w[:, None, :].astype(u.dtype), window_strides=(1,),
        padding=[(CONV_WIDTH - 1, 0)], dimension_numbers=('NWC', 'WIO', 'NWC'),
        feature_group_count=C) + conv_b
    gate, up = jnp.split(u, 2, axis=-1)
    return (jax.nn.gelu(gate, approximate=True) * up) @ w_down


def setup_inputs(seed: int = 0) -> dict:
    key = jax.random.key(seed)
    ks = jax.random.split(key, 17)
    f32 = jnp.float32

    def nrm(k, shape, s):
        return jax.random.normal(k, shape, f32) * s

    def gain(k, shape):
        return 1.0 + 0.02 * jax.random.normal(k, shape, f32)

    return {
        "x": nrm(ks[0], (BATCH, SEQ, D_MODEL), 1.0),
        "meta_tokens": nrm(ks[1], (N_META, D_MODEL), 1.0),
        "rel_bias_table": nrm(ks[2], (REL_BUCKETS, DIFF_HEADS), 0.5),
        "hgrn_lb_logits": nrm(ks[3], (DEPTH + 1, HGRN_KDIM), 0.1),
        "ln_mix_pre": gain(ks[4], (DEPTH, D_MODEL)),
        "ln_mix_post": gain(ks[5], (DEPTH, D_MODEL)),
        "w_in": nrm(ks[6], (DEPTH, D_MODEL, IN_COLS), D_MODEL ** -0.5),
        "diff_lambda": nrm(ks[7], (DEPTH, 4, DIFF_QK_DIM), 0.1),
        "diff_subln": gain(ks[8], (DEPTH, DIFF_V_DIM)),
        "hgrn_gnorm": gain(ks[9], (DEPTH, HGRN_HEAD_V)),
        "w_out": nrm(ks[10], (DEPTH, D_MIX, D_MODEL), D_MIX ** -0.5),
        "ln_ffn_pre": gain(ks[11], (DEPTH, D_MODEL)),
        "ln_ffn_post": gain(ks[12], (DEPTH, D_MODEL)),
        "w_ffn_up": nrm(ks[13], (DEPTH, D_MODEL, 2 * D_FF), D_MODEL ** -0.5),
        "ffn_conv_w": nrm(ks[14], (DEPTH, CONV_WIDTH, 2 * D_FF), CONV_WIDTH ** -0.5),
        "ffn_conv_b": nrm(ks[15], (DEPTH, 2 * D_FF), 0.01),
        "w_ffn_down": nrm(ks[16], (DEPTH, D_FF, D_MODEL), D_FF ** -0.5),
    }


def reference(x, meta_tokens, rel_bias_table, hgrn_lb_logits, ln_mix_pre, ln_mix_post,
              w_in, diff_lambda, diff_subln, hgrn_gnorm, w_out, ln_ffn_pre, ln_ffn_post,
              w_ffn_up, ffn_conv_w, ffn_conv_b, w_ffn_down):
    B = x.shape[0]
    meta = jnp.broadcast_to(meta_tokens[None].astype(x.dtype), (B, N_META, D_MODEL))
    h = jnp.concatenate([meta, x], axis=1)
    lb_all = jnp.cumsum(jax.nn.softmax(hgrn_lb_logits.astype(jnp.float32), axis=0), axis=0)
    for l in range(DEPTH):
        a = rmsnorm(h, ln_mix_pre[l])
        proj = a @ w_in[l]
        dq, dk, dv, hq, hf, hi, hg = jnp.split(proj, SPLITS, axis=-1)
        lam_init = 0.8 - 0.6 * math.exp(-0.3 * l)
        y_diff = diff_attention(dq, dk, dv, diff_lambda[l], diff_subln[l], rel_bias_table, lam_init)
        y_hgrn = hgrn2(hq, hf, hi, hg, lb_all[l], hgrn_gnorm[l])
        mix = jnp.concatenate([y_diff, y_hgrn], axis=-1) @ w_out[l]
        h = h + rmsnorm(mix, ln_mix_post[l])
        a = rmsnorm(h, ln_ffn_pre[l])
        y = conv_ffn(a, w_ffn_up[l], ffn_conv_w[l], ffn_conv_b[l], w_ffn_down[l])
        h = h + rmsnorm(y, ln_ffn_post[l])
    return h[:, N_META:, :]
```

```python
import contextlib
import math

import numpy as np
import concourse.bass as bass
import concourse.mybir as mybir
from concourse.bass_utils import run_bass_kernel_spmd

F32 = mybir.dt.float32
BF16 = mybir.dt.bfloat16
AF = mybir.ActivationFunctionType
ALU = mybir.AluOpType
AX = mybir.AxisListType

D = 1024
NT = 17
LP = NT * 128
NPAD = 112
DFF = 2816
NCH = 22
EPS = 1e-6
LAM_INIT = 0.2
GROUPS = [(0, 1), (1, 4), (5, 4), (9, 4), (13, 4)]
WIN_ORDER = ["hf", "hi", "dk", "dv", "dq", "hq", "hg"]
WIN_COL = {"dq": 0, "dk": 512, "dv": 1024, "hq": 1536, "hf": 2048, "hi": 2560, "hg": 3072}
VW = 136
NEG = -30000.0
SEM_EPOCH = 500
STRICT_SAME_ENGINE = True

_C = {}
_off = 0
for _n, _w in [("gpost", 1024), ("gfpost", 1024), ("gprec", 8), ("gfprec", 8), ("cw", 132), ("cb", 44),
               ("lbl", 8), ("lamv", 256), ("subln", 128), ("gnorm", 128), ("biasd", 1024), ("t31", 4),
               ("hmask", 128), ("vvalid", 1), ("ident", 128)]:
    _C[_n] = (_off, _w)
    _off += _w
NCONST = _off


class Op:
    __slots__ = ("eng", "fn", "deps", "sig", "val", "sem", "is_dma", "semkey", "phase")

    def __init__(self, eng, fn, is_dma, semkey):
        self.eng = eng
        self.fn = fn
        self.deps = []
        self.sig = False
        self.val = 0
        self.sem = None
        self.is_dma = is_dma
        self.semkey = semkey


class Prog:
    ENGS = ("pe", "act", "dve", "pool", "sp")

    def __init__(self):
        self.ops = {e: [] for e in self.ENGS}
        self.res_w = {}
        self.res_r = {}
        self.dma_cnt = {}
        self.dma_ops = {}
        self.frozen = False
        self.phase = ""
        self.names = None

    def add(self, eng, fn, reads=(), writes=(), dma=None, force=False):
        if self.frozen and not force:
            return None
        op = Op(eng, fn, dma is not None, dma)
        op.phase = self.phase
        deps = {}
        for r in reads:
            w = self.res_w.get(r)
            if w is not None:
                deps[id(w)] = (w, True)
        for r in writes:
            w = self.res_w.get(r)
            if w is not None and id(w) not in deps:
                deps[id(w)] = (w, False)
            for rd in self.res_r.get(r, ()):
                if id(rd) not in deps:
                    deps[id(rd)] = (rd, False)
        for d, raw in deps.values():
            if d is op:
                continue
            if d.eng == eng and not d.is_dma and not op.is_dma:
                if eng == "pe" or (not raw and not STRICT_SAME_ENGINE):
                    continue
            op.deps.append(d)
        for r in reads:
            self.res_r.setdefault(r, []).append(op)
        for r in writes:
            self.res_w[r] = op
            self.res_r[r] = []
        if op.is_dma:
            c = self.dma_cnt.get(dma, 0) + 16
            self.dma_cnt[dma] = c
            op.val = c
            self.dma_ops.setdefault(dma, []).append(op)
        self.ops[eng].append(op)
        return op

    def seal(self, semkey):
        tot = self.dma_cnt.get(semkey, 0)
        for op in self.dma_ops.get(semkey, []):
            op.val = tot

    def finalize(self, sem_ctx):
        for eng in self.ENGS:
            for op in self.ops[eng]:
                for d in op.deps:
                    d.sig = True
        for eng in self.ENGS:
            c = 0
            for op in self.ops[eng]:
                if op.is_dma:
                    op.sem = sem_ctx("d_" + op.semkey)
                else:
                    if op.sig:
                        c += 1
                    ep = max(c - 1, 0) // SEM_EPOCH
                    op.sem = sem_ctx(f"e_{eng}_{ep}")
                    if op.sig:
                        op.val = c - ep * SEM_EPOCH

    def run(self, eng, e):
        known = {}
        for op in self.ops[eng]:
            best = {}
            for d in op.deps:
                k = id(d.sem)
                if known.get(k, 0) >= d.val:
                    continue
                if k not in best or best[k][1] < d.val:
                    best[k] = (d.sem, d.val)
            for k, (sem, val) in best.items():
                e.wait_ge(sem, val)
                known[k] = val
            ins = op.fn(e)
            if ins is None:
                continue
            if self.names is not None:
                self.names[ins.ins.name] = op.phase
            if op.is_dma:
                ins.then_inc(op.sem, 16)
            elif op.sig:
                ins.then_inc(op.sem, 1)


def mk(method, *args, **kw):
    return lambda e: getattr(e, method)(*args, **kw)


def build_nc(debug=False, stop=None, names=None):
    nc = bass.Bass("TRN2", target_bir_lowering=False)
    xp = nc.dram_tensor("xp", [LP, D], F32, kind="ExternalInput").ap()
    w_in = nc.dram_tensor("w_in", [7 * 128, 4096], F32, kind="ExternalInput").ap()
    w_out = nc.dram_tensor("w_out", [2 * 128, 4096], F32, kind="ExternalInput").ap()
    w_up = nc.dram_tensor("w_up", [11 * 128, 4096], F32, kind="ExternalInput").ap()
    w_dn = nc.dram_tensor("w_dn", [DFF, D], F32, kind="ExternalInput").ap()
    consts = nc.dram_tensor("consts", [128, NCONST], F32, kind="ExternalInput").ap()
    out = nc.dram_tensor("out", [2048, D], F32, kind="ExternalOutput").ap()
    if debug:
        dbg_h2 = nc.dram_tensor("dbg_h2", [LP, D], F32, kind="ExternalOutput").ap()
        dbg_y = nc.dram_tensor("dbg_y", [LP, D], BF16, kind="ExternalOutput").ap()

    w_in_u = w_in.rearrange("(u p) f -> u p f", p=128)
    w_out_u = w_out.rearrange("(u p) f -> u p f", p=128)
    w_up_u = w_up.rearrange("(u p) f -> u p f", p=128)
    w_dn_v = w_dn.rearrange("(c p) d -> p c d", p=128)
    w_dn_bf = nc.dram_tensor("w_dn_bf", [6 * 128, 4096], BF16).ap()
    w_dn_bf_u = w_dn_bf.rearrange("(u p) f -> u p f", p=128)

    es = contextlib.ExitStack()
    with es:
        def sb(name, shape, dt=F32):
            return es.enter_context(nc.sbuf_tensor(name, shape, dt))

        cst = sb("cst", [128, NCONST])

        def C(name):
            o, w = _C[name]
            return cst[:, o:o + w]

        identb = sb("identb", [128, 128], BF16)
        sm = sb("sm", [128, 64])
        lb_c, oml_c = sm[:, 0:4], sm[:, 4:8]
        lam_t = sm[:, 8:12]
        nlam = sm[:, 12:13]
        subln8 = sb("subln8", [128, 128])
        Kc = sb("Kc", [128, 4, LP], BF16)
        Vc = sb("Vc", [128, NT, 4, VW], BF16)
        NS = 3
        ws = [sb(f"ws{i}", [128, 4096], BF16) for i in range(NS)]
        h2 = sb("h2", [128, 4, D])
        aT = sb("aT", [128, 8, 512], BF16)
        atok = [sb(f"atok{i}", [128, D], BF16) for i in range(2)]
        junk = sb("junk", [128, D], BF16)
        junkd = sb("junkd", [128, 128], BF16)
        dqT = sb("dqT", [128, 4, 512], BF16)
        qg = sb("qg", [128, 4, 512], BF16)
        kg = sb("kg", [128, 4, 512], BF16)
        kdT = [sb(f"kdT{i}", [128, 512], BF16) for i in range(2)]
        kd_tok = sb("kd_tok", [128, 4, 512], BF16)
        hv = sb("hv", [128, 4, 512], BF16)
        eGb = sb("eGb", [128, 4, 512], BF16)
        eGl = sb("eGl", [128, 4, 4])
        PTt = sb("PTt", [128, 6, 512], BF16)
        Osb = sb("Osb", [128, 9, VW])
        rr = sb("rr", [128, 16])
        odt = sb("odt", [128, 4, 128])
        st_s = sb("st_s", [128, 64])
        y_tok = sb("y_tok", [128, 4, D], BF16)
        Am = sb("Am", [128, 4, 128], BF16)
        S = sb("S", [128, 4, 128])
        Sbf = sb("Sbf", [128, 4, 128], BF16)
        tmpH = sb("tmpH", [128, 512])
        gsb = sb("gsb", [128, 512])
        actT = sb("actT", [128, NCH, 512], BF16)
        fwA = [sb(f"fwA{i}", [128, 3, 512]) for i in range(2)]
        fwU = [sb(f"fwU{i}", [128, 2, 514]) for i in range(2)]
        hw = fwA[1]
        HWR = [("fwA", 1, i) for i in range(3)]
        gw = fwA[0][:, 2, :]
        GWR = ("fwA", 0, 2)
        halo = sb("halo", [128, 2 * NCH, 2])
        a2h = sb("a2h", [128, 8, 2], BF16)
        ones128 = sb("ones128", [128, 128])
        ps = es.enter_context(nc.psum_tensor("ps", [128, 8, 512], F32))
        pst = ps[:, 7, :].bitcast(BF16)

        P = Prog()
        P.names = names
        gring = [0]

        def stage(name):
            if stop is not None and name == stop:
                P.frozen = True

        ring_allowed = [[0, 1, 2, 3]]

        def gbank():
            al = ring_allowed[0]
            b = al[gring[0] % len(al)]
            gring[0] += 1
            return b

        def gpair():
            assert ring_allowed[0] == [0, 1, 2, 3]
            if gring[0] % 2 == 1:
                gring[0] += 1
            b = gring[0] % 4
            gring[0] += 2
            return b

        P.add("sp", mk("dma_start", out=cst[:], in_=consts), writes=["cst"], dma="cst")
        P.add("dve", mk("tensor_copy", out=identb[:], in_=C("ident")), reads=["cst"], writes=["identb"])
        lbl = C("lbl")
        P.add("dve", mk("tensor_tensor", out=lb_c, in0=lbl[:, 4:8], in1=lbl[:, 0:4], op=ALU.subtract),
              reads=["cst"], writes=["lb"])
        P.add("act", mk("activation", out=lb_c, in_=lb_c, func=AF.Exp), reads=["lb"], writes=["lb"])
        P.add("dve", mk("tensor_scalar", out=lb_c, in0=lb_c, scalar1=1.0, scalar2=None, op0=ALU.add),
              reads=["lb"], writes=["lb"])
        P.add("dve", mk("reciprocal", out=lb_c, in_=lb_c), reads=["lb"], writes=["lb"])
        P.add("dve", mk("tensor_scalar", out=oml_c, in0=lb_c, scalar1=-1.0, scalar2=1.0, op0=ALU.mult, op1=ALU.add),
              reads=["lb"], writes=["oml"])
        omlh_c, lbp_c = sm[:, 16:20], sm[:, 20:24]
        P.add("dve", mk("tensor_scalar", out=omlh_c, in0=oml_c, scalar1=0.5, scalar2=None, op0=ALU.mult),
              reads=["oml"], writes=["omlh"])
        P.add("dve", mk("tensor_tensor", out=lbp_c, in0=lb_c, in1=omlh_c, op=ALU.add),
              reads=["lb", "omlh"], writes=["lbp"])
        biasd8 = sb("biasd8", [128, 4, 2, 128], BF16)
        bd4 = C("biasd").rearrange("p (h k q) -> p h k q", h=4, k=2)
        for h_ in range(4):
            P.add("dve", mk("tensor_scalar", out=biasd8[:, h_, :, :], in0=bd4[:, h_, :, :],
                            scalar1=C("t31")[:, h_:h_ + 1], scalar2=8.0, op0=ALU.subtract, op1=ALU.mult),
                  reads=["cst"], writes=["biasd8"])
        lamv = C("lamv")
        for i in range(2):
            P.add("dve", mk("tensor_tensor", out=odt[:, 0, 0:64], in0=lamv[:, (2 * i) * 64:(2 * i + 1) * 64],
                            in1=lamv[:, (2 * i + 1) * 64:(2 * i + 2) * 64], op=ALU.mult),
                  reads=["cst"], writes=["odt"])
            P.add("dve", mk("reduce_sum", out=lam_t[:, i:i + 1], in_=odt[:, 0, 0:64], axis=AX.X),
                  reads=["odt"], writes=[("lam", i)])
        P.add("act", mk("activation", out=lam_t[:, 2:4], in_=lam_t[:, 0:2], func=AF.Exp),
              reads=[("lam", 0), ("lam", 1)], writes=["lame"])
        P.add("dve", mk("tensor_tensor", out=nlam, in0=lam_t[:, 3:4], in1=lam_t[:, 2:3], op=ALU.subtract),
              reads=["lame"], writes=["nlam"])
        P.add("dve", mk("tensor_scalar", out=nlam, in0=nlam, scalar1=-LAM_INIT, scalar2=None, op0=ALU.add),
              reads=["nlam"], writes=["nlam"])
        P.add("dve", mk("tensor_scalar", out=subln8[:], in0=C("subln"), scalar1=1.0 - LAM_INIT, scalar2=None,
                        op0=ALU.mult), reads=["cst"], writes=["subln8"])
        P.add("pool", mk("memset", Vc[:], 0.0), writes=["Vc_ones"])
        P.add("pool", mk("memset", Vc[:, :, :, 128:129], 1.0), reads=["Vc_ones"], writes=["Vc_ones"])
        P.add("dve", mk("tensor_copy", out=Vc[:, 0, :, 128:129],
                        in_=C("vvalid").unsqueeze(1).broadcast_to([128, 4, 1])),
              reads=["cst", "Vc_ones"], writes=["Vc_ones"])
        P.add("pool", mk("memset", S[:], 0.0), writes=["S"])
        P.add("pool", mk("memset", Sbf[:], 0.0), writes=["Sbf"])
        P.add("pool", mk("memset", halo[:], 0.0), writes=["halo_all"])
        P.add("pool", mk("memset", ones128[:], 1.0), writes=["ones"])

        stage('setup')
        wcnt = [0]

        def vgate(t):
            return t[:, 0:2048].rearrange("p (k c) -> p k c", k=8)

        def vup(t):
            return t[:, 2048:4096].rearrange("p (k c) -> p k c", k=8)

        def mk_vd(ncq):
            return lambda t: t[:, 0:ncq * 1024].rearrange("p (c d) -> p c d", c=ncq)

        units = []
        uidx = {}
        for gi_, (t0_, nt_) in enumerate(GROUPS):
            for name in WIN_ORDER:
                c0 = WIN_COL[name]
                uidx[(gi_, name)] = len(units)
                units.append([(lambda t: t[:, 0:4096], w_in_u[c0 // 512])])
            for dh in range(2):
                uidx[(gi_, "wo", dh)] = len(units)
                units.append([(lambda t: t[:, 0:4096], w_out_u[dh])])
            for ju in range(11 if gi_ > 0 else 0):
                uidx[(gi_, "up", ju)] = len(units)
                units.append([(lambda t: t[:, 0:4096], w_up_u[ju])])
            if gi_ > 0:
                for half_ in range(2):
                    for q in range(6):
                        c0, c1 = 4 * q, min(4 * q + 4, NCH)
                        uidx[(gi_, "dn", half_, q)] = len(units)
                        units.append([(lambda t, n_=(c1 - c0) * 1024: t[:, 0:n_], w_dn_bf_u[q][:, 0:(c1 - c0) * 1024],
                                       [("wdnbf", q)])])
        issued = [0]
        LOOK = 2

        def use(key):
            i = uidx[key]
            while issued[0] <= min(i + LOOK, len(units) - 1):
                u = issued[0]
                sidx = u % NS
                for part in units[u]:
                    view_fn, src = part[0], part[1]
                    rds = part[2] if len(part) > 2 else []
                    P.add("pool", mk("dma_start", out=view_fn(ws[sidx]), in_=src), reads=rds, writes=[("ws", sidx)],
                          dma=f"w{sidx}")
                issued[0] += 1
            return i % NS

        def slot(key):
            return uidx[key] % NS

        def v8(t):
            return t[:, 0:4096].rearrange("p (k c) -> p k c", k=8)

        def rstd_ops(ss_ap, out_ap, scale, res_in, res_out):
            P.add("act", mk("activation", out=out_ap, in_=ss_ap, func=AF.Ln, scale=scale, bias=EPS),
                  reads=res_in, writes=res_out)
            P.add("act", mk("activation", out=out_ap, in_=out_ap, func=AF.Exp, scale=-0.5),
                  reads=res_out, writes=res_out)

        def bfbank(b):
            return ps[:, b, :].bitcast(BF16)

        def transpose8(src_tile, src_res, dst_ap3, dst_res, gain_col=None):
            b = gbank()
            pb = bfbank(b)
            for kc in range(8):
                P.add("pe", mk("transpose", out=pb[:, kc * 128:(kc + 1) * 128],
                               in_=src_tile[:, kc * 128:(kc + 1) * 128], identity=identb[:]),
                      reads=list(src_res) + ["identb"], writes=[("ps", b)])
            pv = pb.rearrange("p (k c) -> p k c", k=8)
            if gain_col is None:
                P.add("dve", mk("tensor_copy", out=dst_ap3, in_=pv), reads=[("ps", b)], writes=list(dst_res))
            else:
                P.add("dve", mk("tensor_tensor", out=dst_ap3, in0=pv,
                                in1=gain_col.unsqueeze(2).broadcast_to([128, 8, 128]), op=ALU.mult),
                      reads=[("ps", b), "cst"], writes=list(dst_res))

        biasd = C("biasd").rearrange("p (h k q) -> p h k q", h=4, k=2)
        t31 = C("t31")
        hmask = C("hmask")
        gnorm = C("gnorm")
        cw = C("cw").rearrange("p (c r) -> p c r", r=3)
        cb = C("cb")
        gfpost = C("gfpost")

        def xstage():
            xs = [fwU[0][:].rearrange("p a b -> p (a b)")[:, 0:D], fwU[1][:].rearrange("p a b -> p (a b)")[:, 0:D],
                  fwA[0][:, 0:2, :].rearrange("p a b -> p (a b)"), fwA[1][:, 0:2, :].rearrange("p a b -> p (a b)")]
            xr = [[("fwU", 0, 0), ("fwU", 0, 1)], [("fwU", 1, 0), ("fwU", 1, 1)],
                  [("fwA", 0, 0), ("fwA", 0, 1)], [("fwA", 1, 0), ("fwA", 1, 1)]]
            return xs, xr

        def preloadA(g):
            t0_, nt_ = GROUPS[g]
            xs, xr = xstage()
            for il in range(nt_):
                ti = t0_ + il
                P.add("sp", mk("dma_start", out=xs[il], in_=xp[ti * 128:(ti + 1) * 128, :]), writes=xr[il],
                      dma=f"xs{il}")

        def phaseA(g):
            t0_, nt_ = GROUPS[g]
            P.phase = f'g{g}.A'
            xs, xr = xstage()
            for il in range(nt_):
                P.add("act", mk("activation", out=junk[:], in_=xs[il], func=AF.Square, accum_out=st_s[:, il:il + 1]),
                      reads=xr[il], writes=["junk", ("ssA", il)])
            rstd_ops(st_s[:, 0:nt_], st_s[:, 4:4 + nt_], 1.0 / D, [("ssA", il) for il in range(nt_)], ["rsA"])
            for il in range(nt_):
                sl = slice(il * 128, (il + 1) * 128)
                at = atok[il % 2]
                P.add("act", mk("activation", out=at[:], in_=xs[il], func=AF.Copy, scale=st_s[:, 4 + il:5 + il]),
                      reads=xr[il] + ["rsA"], writes=[("atok", il % 2)])
                transpose8(at, [("atok", il % 2)], aT[:, :, sl], [("aT", il)], C("gprec"))

        def reload_h2(g):
            t0_, nt_ = GROUPS[g]
            for il in range(nt_):
                ti = t0_ + il
                P.add("sp", mk("dma_start", out=h2[:, il, :], in_=xp[ti * 128:(ti + 1) * 128, :]),
                      writes=[("h2", il)], dma=f"x{il}")

        for gi, (t0, nt) in enumerate(GROUPS):
            N = nt * 128
            tok0 = t0 * 128
            tl = t0 + nt - 1
            aT_res = [("aT", il) for il in range(nt)]

            if gi == 0:
                preloadA(0)
                phaseA(0)
                reload_h2(0)
            stage(f'g{gi}A')
            P.phase = f'g{gi}.B'
            def load_win(name, gi=gi):
                return use((gi, name))

            def fm_matmuls(s, h, bank, N=N, aT_res=aT_res):
                wv = v8(ws[s])
                for kc in range(8):
                    P.add("pe", mk("matmul", ps[:, bank, 0:N], lhsT=wv[:, kc, h * 128:(h + 1) * 128],
                                   rhs=aT[:, kc, 0:N], start=(kc == 0), stop=(kc == 7)),
                          reads=[("ws", s)] + aT_res, writes=[("ps", bank)])

            def tm_matmuls(s, il, bank):
                wv = v8(ws[s])
                for kc in range(8):
                    P.add("pe", mk("matmul", ps[:, bank, :], lhsT=aT[:, kc, il * 128:(il + 1) * 128],
                                   rhs=wv[:, kc, :], start=(kc == 0), stop=(kc == 7)),
                          reads=[("ws", s), ("aT", il)], writes=[("ps", bank)])

            WS = [fwA[1], fwA[0]]
            WR = [[("fwA", 1, i) for i in range(3)], [("fwA", 0, i) for i in range(3)]]
            hold = {2: (gsb[:, 0:N], "gsb"), 3: (tmpH[:, 0:N], "tmpH")}

            s_hf = load_win("hf")
            hf_bank = {}
            for h in range(4):
                hf_bank[h] = gbank()
                fm_matmuls(s_hf, h, hf_bank[h])

            def c1(h):
                w = h % 2
                dst, dres = (WS[w][:, 0, 0:N], WR[w][0]) if h < 2 else hold[h]
                P.add("act", mk("activation", out=dst, in_=ps[:, hf_bank[h], 0:N], func=AF.Tanh, scale=0.5),
                      reads=[("ps", hf_bank[h])], writes=[dres])

            def c2(h):
                w = h % 2
                t1 = WS[w][:, 0, 0:N]
                src, sres = (t1, WR[w][0]) if h < 2 else hold[h]
                P.add("dve", mk("tensor_scalar", out=t1, in0=src, scalar1=omlh_c[:, h:h + 1], scalar2=lbp_c[:, h:h + 1],
                                op0=ALU.mult, op1=ALU.add),
                      reads=[sres, "lbp", "omlh"], writes=[WR[w][0]])

            def c3(h):
                w = h % 2
                P.add("act", mk("activation", out=WS[w][:, 1, 0:N], in_=WS[w][:, 0, 0:N], func=AF.Ln),
                      reads=[WR[w][0]], writes=[WR[w][1]])

            def c4(h):
                w = h % 2
                t1 = WS[w][:, 0, 0:N]
                P.add("dve", mk("tensor_scalar", out=t1, in0=t1, scalar1=-1.0, scalar2=1.0, op0=ALU.mult, op1=ALU.add),
                      reads=[WR[w][0]], writes=[WR[w][0]])
                for il in range(nt):
                    sl = slice(il * 128, (il + 1) * 128)
                    P.add("dve", mk("tensor_tensor_scan", out=WS[w][:, 2, sl], data0=ones128[:], data1=WS[w][:, 1, sl],
                                    initial=0.0, op0=ALU.mult, op1=ALU.add),
                          reads=[WR[w][1], "ones"], writes=[WR[w][2]])

            def c5(h):
                w = h % 2
                t3, t4 = WS[w][:, 1, 0:N], WS[w][:, 2, 0:N]
                P.add("act", mk("activation", out=eGb[:, h, 0:N], in_=t4, func=AF.Exp),
                      reads=[WR[w][2]], writes=[("eGb", h)])
                P.add("act", mk("activation", out=eGl[:, h, 0:nt],
                                in_=t4.rearrange("p (a b) -> p a b", b=128)[:, :, 127], func=AF.Exp),
                      reads=[WR[w][2]], writes=[("eGl", h)])
                P.add("act", mk("activation", out=t3, in_=t4, func=AF.Exp, scale=-1.0),
                      reads=[WR[w][2]], writes=[WR[w][1]])

            def c6(h):
                w = h % 2
                t1, t3 = WS[w][:, 0, 0:N], WS[w][:, 1, 0:N]
                P.add("dve", mk("tensor_tensor", out=kg[:, h, 0:N], in0=t1, in1=t3, op=ALU.mult),
                      reads=[WR[w][0], WR[w][1]], writes=[("kg", h)])
                kt = kdT[h % 2]
                for il in range(nt):
                    sl = slice(il * 128, (il + 1) * 128)
                    P.add("dve", mk("scalar_tensor_tensor", out=kt[:, sl], in0=WS[w][:, 0, sl],
                                    scalar=eGl[:, h, il:il + 1], in1=WS[w][:, 1, sl], op0=ALU.mult, op1=ALU.mult),
                          reads=[WR[w][0], WR[w][1], ("eGl", h)], writes=[("kdT", h % 2)])

            def c7(h):
                kt = kdT[h % 2]
                bt = gbank()
                pbt = bfbank(bt)
                for il in range(nt):
                    sl = slice(il * 128, (il + 1) * 128)
                    P.add("pe", mk("transpose", out=pbt[:, sl], in_=kt[:, sl], identity=identb[:]),
                          reads=[("kdT", h % 2), "identb"], writes=[("ps", bt)])
                P.add("act", mk("copy", out=kd_tok[:, 0:nt, h * 128:(h + 1) * 128],
                                in_=pbt[:, 0:N].rearrange("p (a b) -> p a b", b=128)),
                      reads=[("ps", bt)], writes=[("kdtok", h)])

            for h in range(4):
                c1(h)
            c2(0); c2(1)
            s_hi = load_win("hi")
            for il in range(nt):
                b = gbank()
                tm_matmuls(s_hi, il, b)
                P.add("act", mk("copy", out=hv[:, il, :], in_=ps[:, b, :]), reads=[("ps", b)], writes=[("hv", il)])
            c3(0); c3(1); c4(0); c4(1)
            s_dk = load_win("dk")
            for h in range(4):
                b = gbank()
                fm_matmuls(s_dk, h, b)
                P.add("act", mk("copy", out=Kc[:, h, tok0:tok0 + N], in_=ps[:, b, 0:N]),
                      reads=[("ps", b)], writes=[("Kc", h)])
            c5(0); c5(1); c6(0); c6(1)
            c2(2); c2(3)
            s_dv = load_win("dv")
            for il in range(nt):
                b = gbank()
                tm_matmuls(s_dv, il, b)
                P.add("act", mk("copy", out=Vc[:, t0 + il, :, 0:128],
                                in_=ps[:, b, :].rearrange("p (h c) -> p h c", h=4)),
                      reads=[("ps", b), "Vc_ones"], writes=[("Vc", t0 + il)])
            c7(0); c7(1)
            c3(2); c3(3); c4(2); c4(3)
            s_dq = load_win("dq")
            for h in range(4):
                b = gbank()
                fm_matmuls(s_dq, h, b)
                P.add("act", mk("copy", out=dqT[:, h, 0:N], in_=ps[:, b, 0:N]), reads=[("ps", b)],
                      writes=[("dqT", h)])
            c5(2); c5(3); c6(2); c6(3)
            s_hq = load_win("hq")
            for h in range(4):
                if h == 2:
                    c7(2); c7(3)
                b = gbank()
                fm_matmuls(s_hq, h, b)
                P.add("dve", mk("scalar_tensor_tensor", out=qg[:, h, 0:N], in0=ps[:, b, 0:N], scalar=128.0 ** -0.5,
                                in1=eGb[:, h, 0:N], op0=ALU.mult, op1=ALU.mult),
                      reads=[("ps", b), ("eGb", h)], writes=[("qg", h)])
            s_hg = load_win("hg")

            stage(f'g{gi}B')
            if gi == 1:
                for q_ in range(6):
                    c0_, c1_ = 4 * q_, min(4 * q_ + 4, NCH)
                    P.add("pool", mk("dma_start",
                                     out=w_dn_bf_u[q_][:, 0:(c1_ - c0_) * 1024].rearrange("p (c d) -> p c d", d=1024),
                                     in_=w_dn_v[:, c0_:c1_, :]),
                          writes=[("wdnbf", q_)], dma=f"wdnbf{q_}")
            P.phase = f'g{gi}.C'
            na = 2 * nt

            def accb(a):
                return 4 + a // 3, (a % 3) * VW

            wvg = v8(ws[s_hg])
            ps7 = ps[:, 7, :]
            hg_state = {}

            def hg_p1(il):
                sl = slice(il * 128, (il + 1) * 128)
                bA = gbank()
                for h in range(4):
                    hs = slice(h * 128, (h + 1) * 128)
                    P.add("pe", mk("matmul", ps[:, bA, hs], lhsT=kg[:, h, sl], rhs=qg[:, h, sl], start=True, stop=True),
                          reads=[("kg", h), ("qg", h)], writes=[("ps", bA)])
                P.add("dve", mk("tensor_tensor", out=Am[:], in0=ps[:, bA, :].rearrange("p (h c) -> p h c", h=4),
                                in1=hmask.unsqueeze(1).broadcast_to([128, 4, 128]), op=ALU.mult),
                      reads=[("ps", bA), "cst"], writes=["Am"])

            def hg_p3a(il, part):
                sl = slice(il * 128, (il + 1) * 128)
                for kc in (2 * part, 2 * part + 1):
                    P.add("pe", mk("matmul", ps7, lhsT=aT[:, kc, sl], rhs=wvg[:, kc, :],
                                   start=(kc == 0), stop=(kc == 7)),
                          reads=[("ws", s_hg), ("aT", il)], writes=[("ps", 7)])

            def hg_p3b(il):
                P.add("act", mk("activation", out=gw, in_=ps7, func=AF.Tanh, scale=0.5),
                      reads=[("ps", 7)], writes=[GWR])
                P.add("dve", mk("tensor_copy", out=gsb[:], in_=ps7), reads=[("ps", 7), GWR], writes=["gsb"])

            def hg_p2a(il):
                sl = slice(il * 128, (il + 1) * 128)
                for h in range(4):
                    hs = slice(h * 128, (h + 1) * 128)
                    P.add("pe", mk("matmul", ps7[:, hs], lhsT=qg[:, h, sl], rhs=Sbf[:, h, :], start=True, stop=False),
                          reads=[("qg", h), "Sbf"], writes=[("ps", 7)])
                    P.add("pe", mk("matmul", ps7[:, hs], lhsT=Am[:, h, :], rhs=hv[:, il, hs], start=False, stop=True),
                          reads=["Am", ("hv", il)], writes=[("ps", 7)])
                bU = gbank()
                for h in range(4):
                    hs = slice(h * 128, (h + 1) * 128)
                    P.add("pe", mk("matmul", ps[:, bU, hs], lhsT=kd_tok[:, il, hs], rhs=hv[:, il, hs],
                                   start=True, stop=True),
                          reads=[("kdtok", h), ("hv", il)], writes=[("ps", bU)])
                for h in range(4):
                    hs = slice(h * 128, (h + 1) * 128)
                    P.add("dve", mk("scalar_tensor_tensor", out=S[:, h, :], in0=S[:, h, :], scalar=eGl[:, h, il:il + 1],
                                    in1=ps[:, bU, hs], op0=ALU.mult, op1=ALU.add),
                          reads=["S", ("eGl", h), ("ps", bU)], writes=["S"])

            def hg_p2b(il):
                P.add("dve", mk("tensor_copy", out=Sbf[:], in_=S[:]), reads=["S"], writes=["Sbf"])
                for h in range(4):
                    hs = slice(h * 128, (h + 1) * 128)
                    P.add("act", mk("activation", out=junk[:, 0:128], in_=ps7[:, hs], func=AF.Square,
                                    accum_out=st_s[:, 12 + h:13 + h]),
                          reads=[("ps", 7)], writes=["junk", ("ssH", h)])

            def hg_p2c(il):
                rstd_ops(st_s[:, 12:16], st_s[:, 16:20], 1.0 / 128, [("ssH", h) for h in range(4)], ["rsH"])

            def hg_p2d(il):
                for h in range(4):
                    hs = slice(h * 128, (h + 1) * 128)
                    P.add("dve", mk("scalar_tensor_tensor", out=tmpH[:, hs], in0=ps7[:, hs],
                                    scalar=st_s[:, 16 + h:17 + h], in1=gnorm, op0=ALU.mult, op1=ALU.mult),
                          reads=[("ps", 7), "rsH", "cst"], writes=["tmpH"])

            def hg_p3c(il):
                P.add("dve", mk("scalar_tensor_tensor", out=gw, in0=gw, scalar=1.0, in1=gsb[:], op0=ALU.add,
                                op1=ALU.mult),
                      reads=[GWR, "gsb"], writes=[GWR])
                P.add("dve", mk("scalar_tensor_tensor", out=y_tok[:, il, 512:1024], in0=gw, scalar=0.5, in1=tmpH[:],
                                op0=ALU.mult, op1=ALU.mult),
                      reads=[GWR, "tmpH"], writes=[("ytok", il, 4)])

            hg_sched = {}
            for il in range(nt):
                hh = il % 4
                for part_ in range(4):
                    hg_sched.setdefault((hh, 1 + part_), []).append(lambda il=il, part_=part_: hg_p3a(il, part_))
                for kk_, fn_ in ((1, hg_p1), (5, hg_p3b), (7, hg_p2a), (9, hg_p2b), (11, hg_p2c), (12, hg_p2d),
                                 (13, hg_p3c)):
                    hg_sched.setdefault((hh, kk_), []).append(lambda il=il, fn_=fn_: fn_(il))

            for h in range(4):
                P.phase = f'g{gi}.C'
                items = [(j, m) for j in range(tl + 1) for m in range(2)]
                info = {}
                SK = 2

                def st_pair(j, h=h, info=info, t0=t0, nt=nt, N=N):
                    il0 = max(j - t0, 0)
                    nq = nt - il0
                    b0 = gpair()
                    bs = [b0, b0 + 1]
                    p0 = 2 * (j % 3)
                    pts = [p0, p0 + 1]
                    for m in range(2):
                        P.add("pe", mk("matmul", ps[:, bs[m], 0:nq * 128],
                                       lhsT=Kc[m * 64:(m + 1) * 64, h, j * 128:(j + 1) * 128],
                                       rhs=dqT[m * 64:(m + 1) * 64, h, il0 * 128:N], start=True, stop=True),
                              reads=[("Kc", h), ("dqT", h)], writes=[("ps", bs[m])])
                    kinds = []
                    for c in range(nq):
                        tq = t0 + il0 + c
                        kinds.append(0 if tq == j else (1 if tq == j + 1 else 2))
                    nd = sum(1 for kk in kinds if kk < 2)
                    psr = [("ps", bs[0]), ("ps", bs[1])]
                    ptr = [("PT", p0), ("PT", p0 + 1)]
                    if nd > 0:
                        k0 = kinds[0]
                        for m in range(2):
                            P.add("pe", mk("matmul", ps[:, bs[m], 0:nd * 128], lhsT=identb[:],
                                           rhs=biasd8[:, h, k0:k0 + nd, :].rearrange("p a b -> p (a b)"),
                                           start=False, stop=True, skip_group_check=True),
                                  reads=["identb", "biasd8"], writes=[("ps", bs[m])])
                    P.add("act", mk("activation", out=PTt[:, p0:p0 + 2, 0:nq * 128], in_=ps[:, b0:b0 + 2, 0:nq * 128],
                                    func=AF.Exp, scale=0.125, bias=t31[:, h:h + 1]),
                          reads=psr + ["cst"], writes=ptr)
                    info[j] = (il0, nq, pts)

                started = set()

                def pv_pair(j, h=h, info=info, t0=t0, nt=nt, started=started):
                    il0, nq, pts = info[j]
                    for m in range(2):
                        pt = pts[m]
                        for c in range(nq):
                            il = il0 + c
                            bank, off = accb(m * nt + il)
                            st = (j == 0 and bank not in started)
                            started.add(bank)
                            P.add("pe", mk("matmul", ps[:, bank, off:off + 129],
                                           lhsT=PTt[:, pt, c * 128:(c + 1) * 128], rhs=Vc[:, j, h, 0:129],
                                           start=st, stop=(j == t0 + il), skip_group_check=True),
                                  reads=[("PT", pt), ("Vc", j), "Vc_ones"], writes=[("ps", bank)])

                npair = tl + 1
                SKP = 2
                nit = 2 * (npair + SKP)
                for kp in range(npair + SKP):
                    if kp < npair:
                        st_pair(kp)
                    if kp - SKP >= 0:
                        pv_pair(kp - SKP)
                    last = (kp == npair + SKP - 1)
                    for kk in ([2 * kp, 2 * kp + 1] if not last else [2 * kp, 2 * kp + 1] + [q for q in range(nit, 20)]):
                        for f in hg_sched.get((h, kk), []):
                            P.phase = f'g{gi}.D'
                            f()
                            P.phase = f'g{gi}.C'
                P.phase = f'g{gi}.Cev'
                nbk = (na + 2) // 3
                for bi in range(nbk):
                    cnt = min(3, na - 3 * bi)
                    P.add("act", mk("copy", out=Osb[:, 3 * bi:3 * bi + cnt, 0:129],
                                    in_=ps[:, 4 + bi, 0:cnt * VW].rearrange("p (a b) -> p a b", b=VW)[:, :, 0:129]),
                          reads=[("ps", 4 + bi)], writes=[("Osb", bi)])
                osr = [("Osb", bi) for bi in range(nbk)]

                def ev1(h=h, osr=osr):
                    P.add("dve", mk("tensor_scalar", out=rr[:, 0:na], in0=Osb[:, 0:na, 128], scalar1=1e-30,
                                    scalar2=None, op0=ALU.add), reads=osr, writes=["rr"])
                    P.add("dve", mk("reciprocal", out=rr[:, 0:na], in_=rr[:, 0:na]), reads=["rr"], writes=["rr"])
                    P.add("dve", mk("tensor_scalar", out=rr[:, nt:na], in0=rr[:, nt:na], scalar1=nlam, scalar2=None,
                                    op0=ALU.mult), reads=["rr", "nlam"], writes=["rr"])
                    for il in range(nt):
                        P.add("dve", mk("tensor_scalar", out=odt[:, il, :], in0=Osb[:, il, 0:128],
                                        scalar1=rr[:, il:il + 1], scalar2=None, op0=ALU.mult),
                              reads=osr + ["rr"], writes=[("odt", il)])
                        P.add("dve", mk("scalar_tensor_tensor", out=odt[:, il, :], in0=Osb[:, nt + il, 0:128],
                                        scalar=rr[:, nt + il:nt + il + 1], in1=odt[:, il, :], op0=ALU.mult,
                                        op1=ALU.add),
                              reads=osr + ["rr", ("odt", il)], writes=[("odt", il)])

                def ev2(h=h):
                    for il in range(nt):
                        P.add("dve", mk("scalar_tensor_tensor", out=junkd[:], in0=odt[:, il, :], scalar=1.0,
                                        in1=odt[:, il, :], op0=ALU.mult, op1=ALU.mult,
                                        accum_out=st_s[:, 8 + il:9 + il]),
                              reads=[("odt", il)], writes=["junkd", ("ssD", il)])

                def ev3(h=h):
                    rstd_ops(st_s[:, 8:8 + nt], st_s[:, 52:52 + nt], 1.0 / 128, [("ssD", il) for il in range(nt)],
                             ["rsD"])

                def ev4(h=h):
                    for il in range(nt):
                        P.add("dve", mk("scalar_tensor_tensor", out=y_tok[:, il, h * 128:(h + 1) * 128],
                                        in0=odt[:, il, :], scalar=st_s[:, 52 + il:53 + il], in1=subln8[:],
                                        op0=ALU.mult, op1=ALU.mult),
                              reads=[("odt", il), "rsD", "subln8"], writes=[("ytok", il, h)])

                if h < 3:
                    for kk_, fn_ in ((2, ev1), (5, ev2), (8, ev3), (12, ev4)):
                        hg_sched.setdefault((h + 1, kk_), []).append(fn_)
                else:
                    ev1(); ev2(); ev3(); ev4()

            stage(f'g{gi}C')
            stage(f'g{gi}D')
            P.phase = f'g{gi}.E'
            yT = actT
            s_wo = [use((gi, "wo", 0)), slot((gi, "wo", 1))]
            wo = [v8(ws[s_wo[0]]), v8(ws[s_wo[1]])]
            gpost3 = C("gpost").rearrange("p (a b) -> p a b", a=2)

            def e_s0(il):
                ti = t0 + il
                sl = slice(il * 128, (il + 1) * 128)
                ytr = [("ytok", il, hh) for hh in range(5)]
                if debug:
                    P.add("sp", mk("dma_start", out=dbg_y[ti * 128:(ti + 1) * 128, :], in_=y_tok[:, il, :]),
                          reads=ytr, writes=[("dbgy", ti)], dma=f"dy{il}")
                transpose8(y_tok[:, il, :], ytr, yT[:, 0:8, sl],
                           [("actT", il)] + [("actT", "c", ci) for ci in range(8)], None)

            def e_s1(il):
                sl = slice(il * 128, (il + 1) * 128)
                b0 = 4 + 2 * (il % 2)
                for kc in range(8):
                    for dh in range(2):
                        P.add("pe", mk("matmul", ps[:, b0 + dh, :], lhsT=yT[:, kc, sl], rhs=wo[dh][:, kc, :],
                                       start=(kc == 0), stop=(kc == 7)),
                              reads=[("actT", il), ("ws", s_wo[dh])], writes=[("ps", b0 + dh)])
                mixv = ps[:, b0:b0 + 2, :]
                P.add("act", mk("activation", out=junk[:].rearrange("p (a b) -> p a b", a=2), in_=mixv,
                                func=AF.Square, accum_out=st_s[:, 20 + il:21 + il]),
                      reads=[("ps", b0), ("ps", b0 + 1)], writes=["junk", ("ssM", il)])
                rstd_ops(st_s[:, 20 + il:21 + il], st_s[:, 24 + il:25 + il], 1.0 / D, [("ssM", il)], [("rsM", il)])

            def e_s2(il):
                ti = t0 + il
                b0 = 4 + 2 * (il % 2)
                mixv = ps[:, b0:b0 + 2, :]
                tmpM = fwA[il % 2][:, 0:2, :]
                tr = [("fwA", il % 2, 0), ("fwA", il % 2, 1)]
                P.add("dve", mk("scalar_tensor_tensor", out=tmpM, in0=mixv, scalar=st_s[:, 24 + il:25 + il],
                                in1=gpost3, op0=ALU.mult, op1=ALU.mult),
                      reads=[("ps", b0), ("ps", b0 + 1), ("rsM", il), "cst"], writes=tr)
                P.add("dve", mk("tensor_tensor", out=h2[:, il, :], in0=h2[:, il, :],
                                in1=tmpM.rearrange("p a b -> p (a b)"), op=ALU.add),
                      reads=[("h2", il)] + tr, writes=[("h2", il)])
                if debug:
                    P.add("sp", mk("dma_start", out=dbg_h2[ti * 128:(ti + 1) * 128, :], in_=h2[:, il, :]),
                          reads=[("h2", il)], writes=[("dbgh", ti)], dma=f"dh{il}")
                P.add("act", mk("activation", out=junk[:], in_=h2[:, il, :], func=AF.Square,
                                accum_out=st_s[:, 28 + il:29 + il]),
                      reads=[("h2", il)], writes=["junk", ("ss2", il)])
                rstd_ops(st_s[:, 28 + il:29 + il], st_s[:, 32 + il:33 + il], 1.0 / D, [("ss2", il)], [("rs2", il)])
                at = atok[il % 2]
                P.add("act", mk("activation", out=at[:], in_=h2[:, il, :], func=AF.Copy, scale=st_s[:, 32 + il:33 + il]),
                      reads=[("h2", il), ("rs2", il)], writes=[("atok", il % 2)])

            def e_s3(il):
                sl = slice(il * 128, (il + 1) * 128)
                transpose8(atok[il % 2], [("atok", il % 2)], aT[:, :, sl], [("aT", il)], C("gfprec"))

            for step in range(nt + 3):
                if step < nt:
                    e_s0(step)
                if 0 <= step - 1 < nt:
                    e_s1(step - 1)
                if 0 <= step - 2 < nt:
                    e_s2(step - 2)
                if 0 <= step - 3 < nt:
                    e_s3(step - 3)

            stage(f'g{gi}E')
            P.phase = f'g{gi}.F'
            def vg(t):
                return t[:, 0:4096].rearrange("p (k c) -> p k c", k=8)[:, :, 0:256]

            def vu(t):
                return t[:, 0:4096].rearrange("p (k c) -> p k c", k=8)[:, :, 256:512]

            def ffn_x(ci, s):
                cc = ci % 2
                wgu = (vg(ws[s]), vu(ws[s]))
                fs = ci % 2
                A, U = fwA[fs], fwU[fs]
                if gi == 1:
                    hbk = 4 + (ci % 4)
                    for half in range(2):
                        for kc in range(8):
                            P.add("pe", mk("matmul", ps[:, hbk, 2 * half:2 * half + 2],
                                           lhsT=wgu[half][:, kc, cc * 128:(cc + 1) * 128], rhs=a2h[:, kc, :],
                                           start=(kc == 0), stop=(kc == 7), skip_group_check=True),
                                  reads=[("ws", s), "a2h"], writes=[("ps", hbk)])
                for half in range(2):
                    b = gbank()
                    cidx = ci + NCH * half
                    yv = A[:, half, 0:N]
                    yr = ("fwA", fs, half)
                    ur = ("fwU", fs, half)
                    for kc in range(8):
                        P.add("pe", mk("matmul", ps[:, b, 0:N], lhsT=wgu[half][:, kc, cc * 128:(cc + 1) * 128],
                                       rhs=aT[:, kc, 0:N], start=(kc == 0), stop=(kc == 7)),
                              reads=[("ws", s)] + aT_res, writes=[("ps", b)])
                    P.add("act", mk("copy", out=U[:, half, 2:2 + N], in_=ps[:, b, 0:N]),
                          reads=[("ps", b)], writes=[ur])
                    P.add("act", mk("activation", out=yv, in_=ps[:, b, 0:N], func=AF.Identity,
                                    scale=cw[:, cidx, 2:3], bias=cb[:, cidx:cidx + 1]),
                          reads=[("ps", b), "cst"], writes=[yr])

            def ffn_t(ci):
                fs = ci % 2
                A, U = fwA[fs], fwU[fs]
                hview = halo[:].rearrange("p (h c) r -> p h c r", h=2)[:, :, ci, :]
                urs = [("fwU", fs, 0), ("fwU", fs, 1)]
                if gi == 1:
                    hbk = 4 + (ci % 4)
                    hsrc = ps[:, hbk, 0:4].rearrange("p (h r) -> p h r", h=2)
                    P.add("act", mk("copy", out=U[:, :, 0:2], in_=hsrc), reads=[("ps", hbk)], writes=urs)
                else:
                    P.add("act", mk("copy", out=U[:, :, 0:2], in_=hview),
                          reads=["halo_all", ("halo", ci)], writes=urs)
                P.add("act", mk("copy", out=hview, in_=U[:, :, N:N + 2]),
                      reads=urs, writes=[("halo", ci)])
                for half in range(2):
                    cidx = ci + NCH * half
                    yv = A[:, half, 0:N]
                    yr = ("fwA", fs, half)
                    ur = ("fwU", fs, half)
                    for r in (1, 0):
                        P.add("dve", mk("scalar_tensor_tensor", out=yv, in0=U[:, half, r:r + N],
                                        scalar=cw[:, cidx, r:r + 1], in1=yv, op0=ALU.mult, op1=ALU.add),
                              reads=[ur, yr, "cst"], writes=[yr])

            def ffn_z(ci):
                fs = ci % 2
                A = fwA[fs]
                yg, yu, sq = A[:, 0, 0:N], A[:, 1, 0:N], A[:, 2, 0:N]
                rg, ru, rq = ("fwA", fs, 0), ("fwA", fs, 1), ("fwA", fs, 2)
                P.add("act", mk("activation", out=sq, in_=yg, func=AF.Gelu_apprx_tanh), reads=[rg], writes=[rq])
                P.add("dve", mk("tensor_tensor", out=actT[:, ci, 0:N], in0=sq, in1=yu, op=ALU.mult),
                      reads=[rq, ru],
                      writes=[("actT", "c", ci)] + ([("actT", il) for il in range(nt)] if ci < 8 else []))

            s_up = None
            if gi == 0:
                P.add("dve", mk("tensor_copy", out=a2h[:], in_=aT[:, :, N - 2:N]), reads=aT_res, writes=["a2h"])
            else:
                for ci in range(NCH + 1):
                    if ci < NCH:
                        if ci % 2 == 0:
                            s_up = use((gi, "up", ci // 2))
                        ffn_x(ci, s_up)
                        ffn_t(ci)
                    if ci >= 1:
                        ffn_z(ci - 1)
            stage(f'g{gi}F')
            if gi == 0:
                preloadA(1)
                phaseA(1)
                reload_h2(1)
                continue
            if gi + 1 < len(GROUPS):
                preloadA(gi + 1)
            P.phase = f'g{gi}.G'
            for half in range(2):
                for q in range(6):
                    c0 = 4 * q
                    c1 = min(c0 + 4, NCH)
                    ncq = c1 - c0
                    s_dn = use((gi, "dn", half, q))
                    wd = ws[s_dn][:, 0:ncq * 1024].rearrange("p (c d) -> p c d", c=ncq)
                    for il in (2 * half, 2 * half + 1):
                        sl = slice(il * 128, (il + 1) * 128)
                        for ci in range(c0, c1):
                            for dh in range(2):
                                bank = dh * 4 + il
                                P.add("pe", mk("matmul", ps[:, bank, :], lhsT=actT[:, ci, sl],
                                               rhs=wd[:, ci - c0, dh * 512:(dh + 1) * 512],
                                               start=(ci == 0), stop=(ci == NCH - 1)),
                                      reads=[("actT", "c", ci), ("ws", s_dn)], writes=[("ps", bank)])

            def final_norm(tiles):
                P.phase = f'g{gi}.H'
                k0 = tiles[0]
                n = len(tiles)
                for il in tiles:
                    for dh in range(2):
                        P.add("act", mk("activation", out=junk[:, 0:512], in_=ps[:, dh * 4 + il, :], func=AF.Square,
                                        accum_out=st_s[:, 36 + 2 * il + dh:37 + 2 * il + dh]),
                              reads=[("ps", dh * 4 + il)], writes=["junk", ("ssF", il, dh)])
                ssv = st_s[:, 36 + 2 * k0:36 + 2 * (k0 + n)].rearrange("p (a b) -> p a b", b=2)
                P.add("dve", mk("tensor_tensor", out=st_s[:, 44 + k0:44 + k0 + n], in0=ssv[:, :, 0], in1=ssv[:, :, 1],
                                op=ALU.add),
                      reads=[("ssF", il, dh) for il in tiles for dh in range(2)], writes=[("ssFt", k0)])
                rstd_ops(st_s[:, 44 + k0:44 + k0 + n], st_s[:, 48 + k0:48 + k0 + n], 1.0 / D, [("ssFt", k0)],
                         [("rsF", k0)])
                tmps = [(gsb[:], "gsb"), (tmpH[:], "tmpH")]
                for il in tiles:
                    ti = t0 + il
                    for dh in range(2):
                        tb, tres = tmps[dh]
                        hs = slice(dh * 512, (dh + 1) * 512)
                        P.add("dve", mk("scalar_tensor_tensor", out=tb, in0=ps[:, dh * 4 + il, :],
                                        scalar=st_s[:, 48 + il:49 + il], in1=gfpost[:, hs], op0=ALU.mult, op1=ALU.mult),
                              reads=[("ps", dh * 4 + il), ("rsF", k0), "cst"], writes=[tres])
                        P.add("dve", mk("tensor_tensor", out=h2[:, il, hs], in0=h2[:, il, hs], in1=tb, op=ALU.add),
                              reads=[("h2", il), tres], writes=[("h2", il)])
                    P.add("sp", mk("dma_start", out=out[(ti - 1) * 128:ti * 128, :], in_=h2[:, il, :]),
                          reads=[("h2", il)], writes=[("out", ti)], dma=f"o{il}")

            final_norm([0, 1])
            if gi + 1 < len(GROUPS):
                ring_allowed[0] = [0, 1]
                phaseA(gi + 1)
                ring_allowed[0] = [0, 1, 2, 3]
            final_norm([2, 3])
            if gi + 1 < len(GROUPS):
                reload_h2(gi + 1)

        fin = [("out", ti) for ti in range(1, NT)]
        if debug:
            fin += [("dbgy", ti) for ti in range(NT)] + [("dbgh", ti) for ti in range(NT)]
        P.add("sp", lambda e: None, reads=fin, force=True)

        semd = {}

        def sem_ctx(name):
            if name not in semd:
                semd[name] = es.enter_context(nc.semaphore(name))
            return semd[name]

        P.finalize(sem_ctx)
        with nc.Block() as block:
            @block.tensor
            def _(e):
                P.run("pe", e)

            @block.scalar
            def _(e):
                P.run("act", e)

            @block.vector
            def _(e):
                P.run("dve", e)

            @block.gpsimd
            def _(e):
                P.run("pool", e)

            @block.sync
            def _(e):
                P.run("sp", e)
    return nc


def _rel_bucket_static(dist):
    n = np.maximum(dist, 0)
    nf = np.maximum(n, 1).astype(np.float32)
    large = 16 + (np.log(nf / np.float32(16)) / np.float32(math.log(128 / 16)) * np.float32(16)).astype(np.int32)
    large = np.minimum(large, 31)
    return np.where(n < 16, n, large)


def _unit_major(w, col_blocks):
    w = np.asarray(w, np.float32)
    out = np.empty((len(col_blocks) * 128, 4096), np.float32)
    for u, cols in enumerate(col_blocks):
        blk = w[:, cols].reshape(8, 128, 512).transpose(1, 0, 2).reshape(128, 4096)
        out[u * 128:(u + 1) * 128] = blk
    return out


def _w_up_layout(w_up):
    idx = np.concatenate([np.concatenate([np.arange(256 * ju, 256 * ju + 256),
                                          DFF + np.arange(256 * ju, 256 * ju + 256)]) for ju in range(11)])
    return _unit_major(w_up, [idx[512 * ju:512 * ju + 512] for ju in range(11)])


def _consts(inp):
    c = np.zeros((128, NCONST), np.float32)

    def put(name, arr):
        o, w = _C[name]
        c[:, o:o + w] = np.asarray(arr, np.float32).reshape(128, w)

    bc = lambda v: np.broadcast_to(np.asarray(v, np.float32)[None, :], (128, len(v)))
    put("gpost", bc(inp["ln_mix_post"][0]))
    put("gfpost", bc(inp["ln_ffn_post"][0]))
    put("gprec", inp["ln_mix_pre"][0].reshape(8, 128).T)
    put("gfprec", inp["ln_ffn_pre"][0].reshape(8, 128).T)
    cwv = inp["ffn_conv_w"][0]
    put("cw", cwv.reshape(3, 44, 128).transpose(2, 1, 0))
    put("cb", inp["ffn_conv_b"][0].reshape(44, 128).T)
    put("lbl", inp["hgrn_lb_logits"].reshape(2, 4, 128).transpose(2, 0, 1))
    put("lamv", bc(inp["diff_lambda"][0].reshape(-1)))
    put("subln", bc(inp["diff_subln"][0]))
    put("gnorm", bc(inp["hgrn_gnorm"][0]))
    tbl = np.asarray(inp["rel_bias_table"], np.float32)
    kl = np.arange(128)[:, None]
    ql = np.arange(128)[None, :]
    bd = np.zeros((128, 4, 2, 128), np.float32)
    for kind in range(2):
        dist = 128 * kind + ql - kl
        bidx = _rel_bucket_static(dist)
        for h in range(4):
            g = tbl[bidx, h]
            bd[:, h, kind, :] = np.where(dist >= 0, g, np.float32(NEG))
    put("biasd", bd)
    put("t31", bc(tbl[31, :]))
    put("hmask", (kl <= ql).astype(np.float32))
    vv = np.ones((128, 1), np.float32)
    vv[:NPAD] = 0.0
    put("vvalid", vv)
    put("ident", np.eye(128, dtype=np.float32))
    return c


def kernel(**inp):
    x = np.asarray(inp["x"], np.float32)
    B = x.shape[0]
    consts = _consts(inp)
    meta = np.asarray(inp["meta_tokens"], np.float32)
    shared = {
        "w_in": _unit_major(inp["w_in"][0], [np.arange(512 * u, 512 * u + 512) for u in range(7)]),
        "w_out": _unit_major(inp["w_out"][0], [np.arange(512 * u, 512 * u + 512) for u in range(2)]),
        "w_up": _w_up_layout(inp["w_ffn_up"][0]),
        "w_dn": np.ascontiguousarray(inp["w_ffn_down"][0], np.float32),
        "consts": consts,
    }
    in_maps = []
    for b in range(B):
        xp = np.zeros((LP, D), np.float32)
        xp[NPAD:128] = meta
        xp[128:] = x[b]
        m = dict(shared)
        m["xp"] = xp
        in_maps.append(m)
    nc = build_nc()
    res = run_bass_kernel_spmd(nc, in_maps, core_ids=list(range(B)))
    return np.stack([np.asarray(r["out"], np.float32).reshape(2048, D) for r in res.results], axis=0)
```

```python
import contextlib
import math

import numpy as np
import concourse.bass as bass
import concourse.mybir as mybir
from concourse.bass_utils import run_bass_kernel_spmd

F32 = mybir.dt.float32
BF16 = mybir.dt.bfloat16
AF = mybir.ActivationFunctionType
ALU = mybir.AluOpType
AX = mybir.AxisListType

D = 1024
NT = 17
LP = NT * 128
NPAD = 112
DFF = 2816
NCH = 22
EPS = 1e-6
LAM_INIT = 0.2
GROUPS = [(0, 1), (1, 4), (5, 4), (9, 4), (13, 4)]
WIN_ORDER = ["hf", "hi", "dk", "dv", "dq", "hq", "hg"]
WIN_COL = {"dq": 0, "dk": 512, "dv": 1024, "hq": 1536, "hf": 2048, "hi": 2560, "hg": 3072}
VW = 136
NEG = -30000.0
SEM_EPOCH = 500
STRICT_SAME_ENGINE = False

_C = {}
_off = 0
for _n, _w in [("gpost", 1024), ("gfpost", 1024), ("gprec", 8), ("gfprec", 8), ("cw", 132), ("cb", 44),
               ("lbl", 8), ("lamv", 256), ("subln", 128), ("gnorm", 128), ("biasd", 1024), ("t31", 4),
               ("hmask", 128), ("vvalid", 1), ("ident", 128)]:
    _C[_n] = (_off, _w)
    _off += _w
NCONST = _off


class Op:
    __slots__ = ("eng", "fn", "deps", "sig", "val", "sem", "is_dma", "semkey", "phase")

    def __init__(self, eng, fn, is_dma, semkey):
        self.eng = eng
        self.fn = fn
        self.deps = []
        self.sig = False
        self.val = 0
        self.sem = None
        self.is_dma = is_dma
        self.semkey = semkey


class Prog:
    ENGS = ("pe", "act", "dve", "pool", "sp")

    def __init__(self):
        self.ops = {e: [] for e in self.ENGS}
        self.res_w = {}
        self.res_r = {}
        self.dma_cnt = {}
        self.dma_ops = {}
        self.frozen = False
        self.phase = ""
        self.names = None

    def add(self, eng, fn, reads=(), writes=(), dma=None, force=False):
        if self.frozen and not force:
            return None
        op = Op(eng, fn, dma is not None, dma)
        op.phase = self.phase
        deps = {}
        for r in reads:
            w = self.res_w.get(r)
            if w is not None:
                deps[id(w)] = (w, True)
        for r in writes:
            w = self.res_w.get(r)
            if w is not None and id(w) not in deps:
                deps[id(w)] = (w, False)
            for rd in self.res_r.get(r, ()):
                if id(rd) not in deps:
                    deps[id(rd)] = (rd, False)
        for d, raw in deps.values():
            if d is op:
                continue
            if d.eng == eng and not d.is_dma and not op.is_dma:
                if eng == "pe" or (not raw and not STRICT_SAME_ENGINE):
                    continue
            op.deps.append(d)
        for r in reads:
            self.res_r.setdefault(r, []).append(op)
        for r in writes:
            self.res_w[r] = op
            self.res_r[r] = []
        if op.is_dma:
            c = self.dma_cnt.get(dma, 0) + 16
            self.dma_cnt[dma] = c
            op.val = c
            self.dma_ops.setdefault(dma, []).append(op)
        self.ops[eng].append(op)
        return op

    def seal(self, semkey):
        tot = self.dma_cnt.get(semkey, 0)
        for op in self.dma_ops.get(semkey, []):
            op.val = tot

    def finalize(self, sem_ctx):
        for eng in self.ENGS:
            for op in self.ops[eng]:
                for d in op.deps:
                    d.sig = True
        for eng in self.ENGS:
            c = 0
            for op in self.ops[eng]:
                if op.is_dma:
                    op.sem = sem_ctx("d_" + op.semkey)
                else:
                    if op.sig:
                        c += 1
                    ep = max(c - 1, 0) // SEM_EPOCH
                    op.sem = sem_ctx(f"e_{eng}_{ep}")
                    if op.sig:
                        op.val = c - ep * SEM_EPOCH

    def run(self, eng, e):
        known = {}
        for op in self.ops[eng]:
            best = {}
            for d in op.deps:
                k = id(d.sem)
                if known.get(k, 0) >= d.val:
                    continue
                if k not in best or best[k][1] < d.val:
                    best[k] = (d.sem, d.val)
            for k, (sem, val) in best.items():
                e.wait_ge(sem, val)
                known[k] = val
            ins = op.fn(e)
            if ins is None:
                continue
            if self.names is not None:
                self.names[ins.ins.name] = op.phase
            if op.is_dma:
                ins.then_inc(op.sem, 16)
            elif op.sig:
                ins.then_inc(op.sem, 1)


def mk(method, *args, **kw):
    return lambda e: getattr(e, method)(*args, **kw)


def build_nc(debug=False, stop=None, names=None):
    nc = bass.Bass("TRN2", target_bir_lowering=False)
    xp = nc.dram_tensor("xp", [LP, D], F32, kind="ExternalInput").ap()
    w_in = nc.dram_tensor("w_in", [7 * 128, 4096], F32, kind="ExternalInput").ap()
    w_out = nc.dram_tensor("w_out", [2 * 128, 4096], F32, kind="ExternalInput").ap()
    w_up = nc.dram_tensor("w_up", [11 * 128, 4096], F32, kind="ExternalInput").ap()
    w_dn = nc.dram_tensor("w_dn", [DFF, D], F32, kind="ExternalInput").ap()
    consts = nc.dram_tensor("consts", [128, NCONST], F32, kind="ExternalInput").ap()
    out = nc.dram_tensor("out", [2048, D], F32, kind="ExternalOutput").ap()
    if debug:
        dbg_h2 = nc.dram_tensor("dbg_h2", [LP, D], F32, kind="ExternalOutput").ap()
        dbg_y = nc.dram_tensor("dbg_y", [LP, D], BF16, kind="ExternalOutput").ap()

    w_in_u = w_in.rearrange("(u p) f -> u p f", p=128)
    w_out_u = w_out.rearrange("(u p) f -> u p f", p=128)
    w_up_u = w_up.rearrange("(u p) f -> u p f", p=128)
    w_dn_v = w_dn.rearrange("(c p) d -> p c d", p=128)
    w_dn_bf = nc.dram_tensor("w_dn_bf", [6 * 128, 4096], BF16).ap()
    w_dn_bf_u = w_dn_bf.rearrange("(u p) f -> u p f", p=128)

    es = contextlib.ExitStack()
    with es:
        def sb(name, shape, dt=F32):
            return es.enter_context(nc.sbuf_tensor(name, shape, dt))

        cst = sb("cst", [128, NCONST])

        def C(name):
            o, w = _C[name]
            return cst[:, o:o + w]

        identb = sb("identb", [128, 128], BF16)
        sm = sb("sm", [128, 64])
        lb_c, oml_c = sm[:, 0:4], sm[:, 4:8]
        lam_t = sm[:, 8:12]
        nlam = sm[:, 12:13]
        subln8 = sb("subln8", [128, 128])
        Kc = sb("Kc", [128, 4, LP], BF16)
        Vc = sb("Vc", [128, NT, 4, VW], BF16)
        NS = 3
        ws = [sb(f"ws{i}", [128, 4096], BF16) for i in range(NS)]
        h2 = sb("h2", [128, 4, D])
        aT = sb("aT", [128, 8, 512], BF16)
        atok = [sb(f"atok{i}", [128, D], BF16) for i in range(2)]
        junk = sb("junk", [128, D], BF16)
        junkd = sb("junkd", [128, 128], BF16)
        dqT = sb("dqT", [128, 4, 512], BF16)
        qg = sb("qg", [128, 4, 512], BF16)
        kg = sb("kg", [128, 4, 512], BF16)
        kdT = [sb(f"kdT{i}", [128, 512], BF16) for i in range(2)]
        kd_tok = sb("kd_tok", [128, 4, 512], BF16)
        hv = sb("hv", [128, 4, 512], BF16)
        eGb = sb("eGb", [128, 4, 512], BF16)
        eGl = sb("eGl", [128, 4, 4])
        PTt = sb("PTt", [128, 6, 512], BF16)
        Osb = sb("Osb", [128, 9, VW])
        rr = sb("rr", [128, 16])
        odt = sb("odt", [128, 4, 128])
        st_s = sb("st_s", [128, 64])
        y_tok = sb("y_tok", [128, 4, D], BF16)
        Am = sb("Am", [128, 4, 128], BF16)
        S = sb("S", [128, 4, 128])
        Sbf = sb("Sbf", [128, 4, 128], BF16)
        tmpH = sb("tmpH", [128, 512])
        gsb = sb("gsb", [128, 512])
        actT = sb("actT", [128, NCH, 512], BF16)
        fwA = [sb(f"fwA{i}", [128, 3, 512]) for i in range(2)]
        fwU = [sb(f"fwU{i}", [128, 2, 514]) for i in range(2)]
        hw = fwA[1]
        HWR = [("fwA", 1, i) for i in range(3)]
        gw = fwA[0][:, 2, :]
        GWR = ("fwA", 0, 2)
        halo = sb("halo", [128, 2 * NCH, 2])
        a2h = sb("a2h", [128, 8, 2], BF16)
        ones128 = sb("ones128", [128, 128])
        ps = es.enter_context(nc.psum_tensor("ps", [128, 8, 512], F32))
        pst = ps[:, 7, :].bitcast(BF16)

        P = Prog()
        P.names = names
        gring = [0]

        def stage(name):
            if stop is not None and name == stop:
                P.frozen = True

        ring_allowed = [[0, 1, 2, 3]]

        def gbank():
            al = ring_allowed[0]
            b = al[gring[0] % len(al)]
            gring[0] += 1
            return b

        def gpair():
            assert ring_allowed[0] == [0, 1, 2, 3]
            if gring[0] % 2 == 1:
                gring[0] += 1
            b = gring[0] % 4
            gring[0] += 2
            return b

        P.add("sp", mk("dma_start", out=cst[:], in_=consts), writes=["cst"], dma="cst")
        P.add("dve", mk("tensor_copy", out=identb[:], in_=C("ident")), reads=["cst"], writes=["identb"])
        lbl = C("lbl")
        P.add("dve", mk("tensor_tensor", out=lb_c, in0=lbl[:, 4:8], in1=lbl[:, 0:4], op=ALU.subtract),
              reads=["cst"], writes=["lb"])
        P.add("act", mk("activation", out=lb_c, in_=lb_c, func=AF.Exp), reads=["lb"], writes=["lb"])
        P.add("dve", mk("tensor_scalar", out=lb_c, in0=lb_c, scalar1=1.0, scalar2=None, op0=ALU.add),
              reads=["lb"], writes=["lb"])
        P.add("dve", mk("reciprocal", out=lb_c, in_=lb_c), reads=["lb"], writes=["lb"])
        P.add("dve", mk("tensor_scalar", out=oml_c, in0=lb_c, scalar1=-1.0, scalar2=1.0, op0=ALU.mult, op1=ALU.add),
              reads=["lb"], writes=["oml"])
        omlh_c, lbp_c = sm[:, 16:20], sm[:, 20:24]
        P.add("dve", mk("tensor_scalar", out=omlh_c, in0=oml_c, scalar1=0.5, scalar2=None, op0=ALU.mult),
              reads=["oml"], writes=["omlh"])
        P.add("dve", mk("tensor_tensor", out=lbp_c, in0=lb_c, in1=omlh_c, op=ALU.add),
              reads=["lb", "omlh"], writes=["lbp"])
        biasd8 = sb("biasd8", [128, 4, 2, 128], BF16)
        bd4 = C("biasd").rearrange("p (h k q) -> p h k q", h=4, k=2)
        for h_ in range(4):
            P.add("dve", mk("tensor_scalar", out=biasd8[:, h_, :, :], in0=bd4[:, h_, :, :],
                            scalar1=C("t31")[:, h_:h_ + 1], scalar2=8.0, op0=ALU.subtract, op1=ALU.mult),
                  reads=["cst"], writes=["biasd8"])
        lamv = C("lamv")
        for i in range(2):
            P.add("dve", mk("tensor_tensor", out=odt[:, 0, 0:64], in0=lamv[:, (2 * i) * 64:(2 * i + 1) * 64],
                            in1=lamv[:, (2 * i + 1) * 64:(2 * i + 2) * 64], op=ALU.mult),
                  reads=["cst"], writes=["odt"])
            P.add("dve", mk("reduce_sum", out=lam_t[:, i:i + 1], in_=odt[:, 0, 0:64], axis=AX.X),
                  reads=["odt"], writes=[("lam", i)])
        P.add("act", mk("activation", out=lam_t[:, 2:4], in_=lam_t[:, 0:2], func=AF.Exp),
              reads=[("lam", 0), ("lam", 1)], writes=["lame"])
        P.add("dve", mk("tensor_tensor", out=nlam, in0=lam_t[:, 3:4], in1=lam_t[:, 2:3], op=ALU.subtract),
              reads=["lame"], writes=["nlam"])
        P.add("dve", mk("tensor_scalar", out=nlam, in0=nlam, scalar1=-LAM_INIT, scalar2=None, op0=ALU.add),
              reads=["nlam"], writes=["nlam"])
        P.add("dve", mk("tensor_scalar", out=subln8[:], in0=C("subln"), scalar1=1.0 - LAM_INIT, scalar2=None,
                        op0=ALU.mult), reads=["cst"], writes=["subln8"])
        P.add("pool", mk("memset", Vc[:], 0.0), writes=["Vc_ones"])
        P.add("pool", mk("memset", Vc[:, :, :, 128:129], 1.0), reads=["Vc_ones"], writes=["Vc_ones"])
        P.add("dve", mk("tensor_copy", out=Vc[:, 0, :, 128:129],
                        in_=C("vvalid").unsqueeze(1).broadcast_to([128, 4, 1])),
              reads=["cst", "Vc_ones"], writes=["Vc_ones"])
        P.add("pool", mk("memset", S[:], 0.0), writes=["S"])
        P.add("pool", mk("memset", Sbf[:], 0.0), writes=["Sbf"])
        P.add("pool", mk("memset", halo[:], 0.0), writes=["halo_all"])
        P.add("pool", mk("memset", ones128[:], 1.0), writes=["ones"])

        stage('setup')
        wcnt = [0]

        def vgate(t):
            return t[:, 0:2048].rearrange("p (k c) -> p k c", k=8)

        def vup(t):
            return t[:, 2048:4096].rearrange("p (k c) -> p k c", k=8)

        def mk_vd(ncq):
            return lambda t: t[:, 0:ncq * 1024].rearrange("p (c d) -> p c d", c=ncq)

        units = []
        uidx = {}
        for gi_, (t0_, nt_) in enumerate(GROUPS):
            for name in WIN_ORDER:
                c0 = WIN_COL[name]
                uidx[(gi_, name)] = len(units)
                units.append([(lambda t: t[:, 0:4096], w_in_u[c0 // 512])])
            for dh in range(2):
                uidx[(gi_, "wo", dh)] = len(units)
                units.append([(lambda t: t[:, 0:4096], w_out_u[dh])])
            for ju in range(11 if gi_ > 0 else 0):
                uidx[(gi_, "up", ju)] = len(units)
                units.append([(lambda t: t[:, 0:4096], w_up_u[ju])])
            if gi_ > 0:
                for half_ in range(2):
                    for q in range(6):
                        c0, c1 = 4 * q, min(4 * q + 4, NCH)
                        uidx[(gi_, "dn", half_, q)] = len(units)
                        units.append([(lambda t, n_=(c1 - c0) * 1024: t[:, 0:n_], w_dn_bf_u[q][:, 0:(c1 - c0) * 1024],
                                       [("wdnbf", q)])])
        issued = [0]
        LOOK = 2

        def use(key):
            i = uidx[key]
            while issued[0] <= min(i + LOOK, len(units) - 1):
                u = issued[0]
                sidx = u % NS
                for part in units[u]:
                    view_fn, src = part[0], part[1]
                    rds = part[2] if len(part) > 2 else []
                    P.add("pool", mk("dma_start", out=view_fn(ws[sidx]), in_=src), reads=rds, writes=[("ws", sidx)],
                          dma=f"w{sidx}")
                issued[0] += 1
            return i % NS

        def slot(key):
            return uidx[key] % NS

        def v8(t):
            return t[:, 0:4096].rearrange("p (k c) -> p k c", k=8)

        def rstd_ops(ss_ap, out_ap, scale, res_in, res_out):
            P.add("act", mk("activation", out=out_ap, in_=ss_ap, func=AF.Ln, scale=scale, bias=EPS),
                  reads=res_in, writes=res_out)
            P.add("act", mk("activation", out=out_ap, in_=out_ap, func=AF.Exp, scale=-0.5),
                  reads=res_out, writes=res_out)

        def bfbank(b):
            return ps[:, b, :].bitcast(BF16)

        def transpose8(src_tile, src_res, dst_ap3, dst_res, gain_col=None):
            b = gbank()
            pb = bfbank(b)
            for kc in range(8):
                P.add("pe", mk("transpose", out=pb[:, kc * 128:(kc + 1) * 128],
                               in_=src_tile[:, kc * 128:(kc + 1) * 128], identity=identb[:]),
                      reads=list(src_res) + ["identb"], writes=[("ps", b)])
            pv = pb.rearrange("p (k c) -> p k c", k=8)
            if gain_col is None:
                P.add("dve", mk("tensor_copy", out=dst_ap3, in_=pv), reads=[("ps", b)], writes=list(dst_res))
            else:
                P.add("dve", mk("tensor_tensor", out=dst_ap3, in0=pv,
                                in1=gain_col.unsqueeze(2).broadcast_to([128, 8, 128]), op=ALU.mult),
                      reads=[("ps", b), "cst"], writes=list(dst_res))

        biasd = C("biasd").rearrange("p (h k q) -> p h k q", h=4, k=2)
        t31 = C("t31")
        hmask = C("hmask")
        gnorm = C("gnorm")
        cw = C("cw").rearrange("p (c r) -> p c r", r=3)
        cb = C("cb")
        gfpost = C("gfpost")

        def xstage():
            xs = [fwU[0][:].rearrange("p a b -> p (a b)")[:, 0:D], fwU[1][:].rearrange("p a b -> p (a b)")[:, 0:D],
                  fwA[0][:, 0:2, :].rearrange("p a b -> p (a b)"), fwA[1][:, 0:2, :].rearrange("p a b -> p (a b)")]
            xr = [[("fwU", 0, 0), ("fwU", 0, 1)], [("fwU", 1, 0), ("fwU", 1, 1)],
                  [("fwA", 0, 0), ("fwA", 0, 1)], [("fwA", 1, 0), ("fwA", 1, 1)]]
            return xs, xr

        def preloadA(g):
            t0_, nt_ = GROUPS[g]
            xs, xr = xstage()
            for il in range(nt_):
                ti = t0_ + il
                P.add("sp", mk("dma_start", out=xs[il], in_=xp[ti * 128:(ti + 1) * 128, :]), writes=xr[il],
                      dma=f"xs{il}")

        def phaseA(g):
            t0_, nt_ = GROUPS[g]
            P.phase = f'g{g}.A'
            xs, xr = xstage()
            for il in range(nt_):
                P.add("act", mk("activation", out=junk[:], in_=xs[il], func=AF.Square, accum_out=st_s[:, il:il + 1]),
                      reads=xr[il], writes=["junk", ("ssA", il)])
            rstd_ops(st_s[:, 0:nt_], st_s[:, 4:4 + nt_], 1.0 / D, [("ssA", il) for il in range(nt_)], ["rsA"])
            for il in range(nt_):
                sl = slice(il * 128, (il + 1) * 128)
                at = atok[il % 2]
                P.add("act", mk("activation", out=at[:], in_=xs[il], func=AF.Copy, scale=st_s[:, 4 + il:5 + il]),
                      reads=xr[il] + ["rsA"], writes=[("atok", il % 2)])
                transpose8(at, [("atok", il % 2)], aT[:, :, sl], [("aT", il)], C("gprec"))

        def reload_h2(g):
            t0_, nt_ = GROUPS[g]
            for il in range(nt_):
                ti = t0_ + il
                P.add("sp", mk("dma_start", out=h2[:, il, :], in_=xp[ti * 128:(ti + 1) * 128, :]),
                      writes=[("h2", il)], dma=f"x{il}")

        for gi, (t0, nt) in enumerate(GROUPS):
            N = nt * 128
            tok0 = t0 * 128
            tl = t0 + nt - 1
            aT_res = [("aT", il) for il in range(nt)]

            if gi == 0:
                preloadA(0)
                phaseA(0)
                reload_h2(0)
            stage(f'g{gi}A')
            P.phase = f'g{gi}.B'
            def load_win(name, gi=gi):
                return use((gi, name))

            def fm_matmuls(s, h, bank, N=N, aT_res=aT_res):
                wv = v8(ws[s])
                for kc in range(8):
                    P.add("pe", mk("matmul", ps[:, bank, 0:N], lhsT=wv[:, kc, h * 128:(h + 1) * 128],
                                   rhs=aT[:, kc, 0:N], start=(kc == 0), stop=(kc == 7)),
                          reads=[("ws", s)] + aT_res, writes=[("ps", bank)])

            def tm_matmuls(s, il, bank):
                wv = v8(ws[s])
                for kc in range(8):
                    P.add("pe", mk("matmul", ps[:, bank, :], lhsT=aT[:, kc, il * 128:(il + 1) * 128],
                                   rhs=wv[:, kc, :], start=(kc == 0), stop=(kc == 7)),
                          reads=[("ws", s), ("aT", il)], writes=[("ps", bank)])

            WS = [fwA[1], fwA[0]]
            WR = [[("fwA", 1, i) for i in range(3)], [("fwA", 0, i) for i in range(3)]]
            hold = {2: (gsb[:, 0:N], "gsb"), 3: (tmpH[:, 0:N], "tmpH")}

            s_hf = load_win("hf")
            hf_bank = {}
            for h in range(4):
                hf_bank[h] = gbank()
                fm_matmuls(s_hf, h, hf_bank[h])

            def c1(h):
                w = h % 2
                dst, dres = (WS[w][:, 0, 0:N], WR[w][0]) if h < 2 else hold[h]
                P.add("act", mk("activation", out=dst, in_=ps[:, hf_bank[h], 0:N], func=AF.Tanh, scale=0.5),
                      reads=[("ps", hf_bank[h])], writes=[dres])

            def c2(h):
                w = h % 2
                t1 = WS[w][:, 0, 0:N]
                src, sres = (t1, WR[w][0]) if h < 2 else hold[h]
                P.add("dve", mk("tensor_scalar", out=t1, in0=src, scalar1=omlh_c[:, h:h + 1], scalar2=lbp_c[:, h:h + 1],
                                op0=ALU.mult, op1=ALU.add),
                      reads=[sres, "lbp", "omlh"], writes=[WR[w][0]])

            def c3(h):
                w = h % 2
                P.add("act", mk("activation", out=WS[w][:, 1, 0:N], in_=WS[w][:, 0, 0:N], func=AF.Ln),
                      reads=[WR[w][0]], writes=[WR[w][1]])

            def c4(h):
                w = h % 2
                t1 = WS[w][:, 0, 0:N]
                P.add("dve", mk("tensor_scalar", out=t1, in0=t1, scalar1=-1.0, scalar2=1.0, op0=ALU.mult, op1=ALU.add),
                      reads=[WR[w][0]], writes=[WR[w][0]])
                for il in range(nt):
                    sl = slice(il * 128, (il + 1) * 128)
                    P.add("dve", mk("tensor_tensor_scan", out=WS[w][:, 2, sl], data0=ones128[:], data1=WS[w][:, 1, sl],
                                    initial=0.0, op0=ALU.mult, op1=ALU.add),
                          reads=[WR[w][1], "ones"], writes=[WR[w][2]])

            def c5(h):
                w = h % 2
                t3, t4 = WS[w][:, 1, 0:N], WS[w][:, 2, 0:N]
                P.add("act", mk("activation", out=eGb[:, h, 0:N], in_=t4, func=AF.Exp),
                      reads=[WR[w][2]], writes=[("eGb", h)])
                P.add("act", mk("activation", out=eGl[:, h, 0:nt],
                                in_=t4.rearrange("p (a b) -> p a b", b=128)[:, :, 127], func=AF.Exp),
                      reads=[WR[w][2]], writes=[("eGl", h)])
                P.add("act", mk("activation", out=t3, in_=t4, func=AF.Exp, scale=-1.0),
                      reads=[WR[w][2]], writes=[WR[w][1]])

            def c6(h):
                w = h % 2
                t1, t3 = WS[w][:, 0, 0:N], WS[w][:, 1, 0:N]
                P.add("dve", mk("tensor_tensor", out=kg[:, h, 0:N], in0=t1, in1=t3, op=ALU.mult),
                      reads=[WR[w][0], WR[w][1]], writes=[("kg", h)])
                kt = kdT[h % 2]
                for il in range(nt):
                    sl = slice(il * 128, (il + 1) * 128)
                    P.add("dve", mk("scalar_tensor_tensor", out=kt[:, sl], in0=WS[w][:, 0, sl],
                                    scalar=eGl[:, h, il:il + 1], in1=WS[w][:, 1, sl], op0=ALU.mult, op1=ALU.mult),
                          reads=[WR[w][0], WR[w][1], ("eGl", h)], writes=[("kdT", h % 2)])

            def c7(h):
                kt = kdT[h % 2]
                bt = gbank()
                pbt = bfbank(bt)
                for il in range(nt):
                    sl = slice(il * 128, (il + 1) * 128)
                    P.add("pe", mk("transpose", out=pbt[:, sl], in_=kt[:, sl], identity=identb[:]),
                          reads=[("kdT", h % 2), "identb"], writes=[("ps", bt)])
                P.add("act", mk("copy", out=kd_tok[:, 0:nt, h * 128:(h + 1) * 128],
                                in_=pbt[:, 0:N].rearrange("p (a b) -> p a b", b=128)),
                      reads=[("ps", bt)], writes=[("kdtok", h)])

            for h in range(4):
                c1(h)
            c2(0); c2(1)
            s_hi = load_win("hi")
            for il in range(nt):
                b = gbank()
                tm_matmuls(s_hi, il, b)
                P.add("act", mk("copy", out=hv[:, il, :], in_=ps[:, b, :]), reads=[("ps", b)], writes=[("hv", il)])
            c3(0); c3(1); c4(0); c4(1)
            s_dk = load_win("dk")
            for h in range(4):
                b = gbank()
                fm_matmuls(s_dk, h, b)
                P.add("act", mk("copy", out=Kc[:, h, tok0:tok0 + N], in_=ps[:, b, 0:N]),
                      reads=[("ps", b)], writes=[("Kc", h)])
            c5(0); c5(1); c6(0); c6(1)
            c2(2); c2(3)
            s_dv = load_win("dv")
            for il in range(nt):
                b = gbank()
                tm_matmuls(s_dv, il, b)
                P.add("act", mk("copy", out=Vc[:, t0 + il, :, 0:128],
                                in_=ps[:, b, :].rearrange("p (h c) -> p h c", h=4)),
                      reads=[("ps", b), "Vc_ones"], writes=[("Vc", t0 + il)])
            c7(0); c7(1)
            c3(2); c3(3); c4(2); c4(3)
            s_dq = load_win("dq")
            for h in range(4):
                b = gbank()
                fm_matmuls(s_dq, h, b)
                P.add("act", mk("copy", out=dqT[:, h, 0:N], in_=ps[:, b, 0:N]), reads=[("ps", b)],
                      writes=[("dqT", h)])
            c5(2); c5(3); c6(2); c6(3)
            s_hq = load_win("hq")
            for h in range(4):
                if h == 2:
                    c7(2); c7(3)
                b = gbank()
                fm_matmuls(s_hq, h, b)
                P.add("dve", mk("scalar_tensor_tensor", out=qg[:, h, 0:N], in0=ps[:, b, 0:N], scalar=128.0 ** -0.5,
                                in1=eGb[:, h, 0:N], op0=ALU.mult, op1=ALU.mult),
                      reads=[("ps", b), ("eGb", h)], writes=[("qg", h)])
            s_hg = load_win("hg")

            stage(f'g{gi}B')
            if gi == 1:
                for q_ in range(6):
                    c0_, c1_ = 4 * q_, min(4 * q_ + 4, NCH)
                    P.add("pool", mk("dma_start",
                                     out=w_dn_bf_u[q_][:, 0:(c1_ - c0_) * 1024].rearrange("p (c d) -> p c d", d=1024),
                                     in_=w_dn_v[:, c0_:c1_, :]),
                          writes=[("wdnbf", q_)], dma=f"wdnbf{q_}")
            P.phase = f'g{gi}.C'
            na = 2 * nt

            def accb(a):
                return 4 + a // 3, (a % 3) * VW

            wvg = v8(ws[s_hg])
            ps7 = ps[:, 7, :]
            hg_state = {}

            def hg_p1(il):
                sl = slice(il * 128, (il + 1) * 128)
                bA = gbank()
                for h in range(4):
                    hs = slice(h * 128, (h + 1) * 128)
                    P.add("pe", mk("matmul", ps[:, bA, hs], lhsT=kg[:, h, sl], rhs=qg[:, h, sl], start=True, stop=True),
                          reads=[("kg", h), ("qg", h)], writes=[("ps", bA)])
                P.add("dve", mk("tensor_tensor", out=Am[:], in0=ps[:, bA, :].rearrange("p (h c) -> p h c", h=4),
                                in1=hmask.unsqueeze(1).broadcast_to([128, 4, 128]), op=ALU.mult),
                      reads=[("ps", bA), "cst"], writes=["Am"])

            def hg_p3a(il, part):
                sl = slice(il * 128, (il + 1) * 128)
                for kc in (2 * part, 2 * part + 1):
                    P.add("pe", mk("matmul", ps7, lhsT=aT[:, kc, sl], rhs=wvg[:, kc, :],
                                   start=(kc == 0), stop=(kc == 7)),
                          reads=[("ws", s_hg), ("aT", il)], writes=[("ps", 7)])

            def hg_p3b(il):
                P.add("act", mk("activation", out=gw, in_=ps7, func=AF.Tanh, scale=0.5),
                      reads=[("ps", 7)], writes=[GWR])
                P.add("dve", mk("tensor_copy", out=gsb[:], in_=ps7), reads=[("ps", 7), GWR], writes=["gsb"])

            def hg_p2a(il):
                sl = slice(il * 128, (il + 1) * 128)
                for h in range(4):
                    hs = slice(h * 128, (h + 1) * 128)
                    P.add("pe", mk("matmul", ps7[:, hs], lhsT=qg[:, h, sl], rhs=Sbf[:, h, :], start=True, stop=False),
                          reads=[("qg", h), "Sbf"], writes=[("ps", 7)])
                    P.add("pe", mk("matmul", ps7[:, hs], lhsT=Am[:, h, :], rhs=hv[:, il, hs], start=False, stop=True),
                          reads=["Am", ("hv", il)], writes=[("ps", 7)])
                bU = gbank()
                for h in range(4):
                    hs = slice(h * 128, (h + 1) * 128)
                    P.add("pe", mk("matmul", ps[:, bU, hs], lhsT=kd_tok[:, il, hs], rhs=hv[:, il, hs],
                                   start=True, stop=True),
                          reads=[("kdtok", h), ("hv", il)], writes=[("ps", bU)])
                for h in range(4):
                    hs = slice(h * 128, (h + 1) * 128)
                    P.add("dve", mk("scalar_tensor_tensor", out=S[:, h, :], in0=S[:, h, :], scalar=eGl[:, h, il:il + 1],
                                    in1=ps[:, bU, hs], op0=ALU.mult, op1=ALU.add),
                          reads=["S", ("eGl", h), ("ps", bU)], writes=["S"])

            def hg_p2b(il):
                P.add("dve", mk("tensor_copy", out=Sbf[:], in_=S[:]), reads=["S"], writes=["Sbf"])
                for h in range(4):
                    hs = slice(h * 128, (h + 1) * 128)
                    P.add("act", mk("activation", out=junk[:, 0:128], in_=ps7[:, hs], func=AF.Square,
                                    accum_out=st_s[:, 12 + h:13 + h]),
                          reads=[("ps", 7)], writes=["junk", ("ssH", h)])

            def hg_p2c(il):
                rstd_ops(st_s[:, 12:16], st_s[:, 16:20], 1.0 / 128, [("ssH", h) for h in range(4)], ["rsH"])

            def hg_p2d(il):
                for h in range(4):
                    hs = slice(h * 128, (h + 1) * 128)
                    P.add("dve", mk("scalar_tensor_tensor", out=tmpH[:, hs], in0=ps7[:, hs],
                                    scalar=st_s[:, 16 + h:17 + h], in1=gnorm, op0=ALU.mult, op1=ALU.mult),
                          reads=[("ps", 7), "rsH", "cst"], writes=["tmpH"])

            def hg_p3c(il):
                P.add("dve", mk("scalar_tensor_tensor", out=gw, in0=gw, scalar=1.0, in1=gsb[:], op0=ALU.add,
                                op1=ALU.mult),
                      reads=[GWR, "gsb"], writes=[GWR])
                P.add("dve", mk("scalar_tensor_tensor", out=y_tok[:, il, 512:1024], in0=gw, scalar=0.5, in1=tmpH[:],
                                op0=ALU.mult, op1=ALU.mult),
                      reads=[GWR, "tmpH"], writes=[("ytok", il, 4)])

            hg_sched = {}
            for il in range(nt):
                hh = il % 4
                for part_ in range(4):
                    hg_sched.setdefault((hh, 1 + part_), []).append(lambda il=il, part_=part_: hg_p3a(il, part_))
                for kk_, fn_ in ((1, hg_p1), (5, hg_p3b), (7, hg_p2a), (9, hg_p2b), (11, hg_p2c), (12, hg_p2d),
                                 (13, hg_p3c)):
                    hg_sched.setdefault((hh, kk_), []).append(lambda il=il, fn_=fn_: fn_(il))

            for h in range(4):
                P.phase = f'g{gi}.C'
                items = [(j, m) for j in range(tl + 1) for m in range(2)]
                info = {}
                SK = 2

                def st_pair(j, h=h, info=info, t0=t0, nt=nt, N=N):
                    il0 = max(j - t0, 0)
                    nq = nt - il0
                    b0 = gpair()
                    bs = [b0, b0 + 1]
                    p0 = 2 * (j % 3)
                    pts = [p0, p0 + 1]
                    for m in range(2):
                        P.add("pe", mk("matmul", ps[:, bs[m], 0:nq * 128],
                                       lhsT=Kc[m * 64:(m + 1) * 64, h, j * 128:(j + 1) * 128],
                                       rhs=dqT[m * 64:(m + 1) * 64, h, il0 * 128:N], start=True, stop=True),
                              reads=[("Kc", h), ("dqT", h)], writes=[("ps", bs[m])])
                    kinds = []
                    for c in range(nq):
                        tq = t0 + il0 + c
                        kinds.append(0 if tq == j else (1 if tq == j + 1 else 2))
                    nd = sum(1 for kk in kinds if kk < 2)
                    psr = [("ps", bs[0]), ("ps", bs[1])]
                    ptr = [("PT", p0), ("PT", p0 + 1)]
                    if nd > 0:
                        k0 = kinds[0]
                        for m in range(2):
                            P.add("pe", mk("matmul", ps[:, bs[m], 0:nd * 128], lhsT=identb[:],
                                           rhs=biasd8[:, h, k0:k0 + nd, :].rearrange("p a b -> p (a b)"),
                                           start=False, stop=True, skip_group_check=True),
                                  reads=["identb", "biasd8"], writes=[("ps", bs[m])])
                    P.add("act", mk("activation", out=PTt[:, p0:p0 + 2, 0:nq * 128], in_=ps[:, b0:b0 + 2, 0:nq * 128],
                                    func=AF.Exp, scale=0.125, bias=t31[:, h:h + 1]),
                          reads=psr + ["cst"], writes=ptr)
                    info[j] = (il0, nq, pts)

                started = set()

                def pv_pair(j, h=h, info=info, t0=t0, nt=nt, started=started):
                    il0, nq, pts = info[j]
                    for m in range(2):
                        pt = pts[m]
                        for c in range(nq):
                            il = il0 + c
                            bank, off = accb(m * nt + il)
                            st = (j == 0 and bank not in started)
                            started.add(bank)
                            P.add("pe", mk("matmul", ps[:, bank, off:off + 129],
                                           lhsT=PTt[:, pt, c * 128:(c + 1) * 128], rhs=Vc[:, j, h, 0:129],
                                           start=st, stop=(j == t0 + il), skip_group_check=True),
                                  reads=[("PT", pt), ("Vc", j), "Vc_ones"], writes=[("ps", bank)])

                npair = tl + 1
                SKP = 2
                nit = 2 * (npair + SKP)
                for kp in range(npair + SKP):
                    if kp < npair:
                        st_pair(kp)
                    if kp - SKP >= 0:
                        pv_pair(kp - SKP)
                    last = (kp == npair + SKP - 1)
                    for kk in ([2 * kp, 2 * kp + 1] if not last else [2 * kp, 2 * kp + 1] + [q for q in range(nit, 20)]):
                        for f in hg_sched.get((h, kk), []):
                            P.phase = f'g{gi}.D'
                            f()
                            P.phase = f'g{gi}.C'
                P.phase = f'g{gi}.Cev'
                nbk = (na + 2) // 3
                for bi in range(nbk):
                    cnt = min(3, na - 3 * bi)
                    P.add("act", mk("copy", out=Osb[:, 3 * bi:3 * bi + cnt, 0:129],
                                    in_=ps[:, 4 + bi, 0:cnt * VW].rearrange("p (a b) -> p a b", b=VW)[:, :, 0:129]),
                          reads=[("ps", 4 + bi)], writes=[("Osb", bi)])
                osr = [("Osb", bi) for bi in range(nbk)]

                def ev1(h=h, osr=osr):
                    P.add("dve", mk("tensor_scalar", out=rr[:, 0:na], in0=Osb[:, 0:na, 128], scalar1=1e-30,
                                    scalar2=None, op0=ALU.add), reads=osr, writes=["rr"])
                    P.add("dve", mk("reciprocal", out=rr[:, 0:na], in_=rr[:, 0:na]), reads=["rr"], writes=["rr"])
                    P.add("dve", mk("tensor_scalar", out=rr[:, nt:na], in0=rr[:, nt:na], scalar1=nlam, scalar2=None,
                                    op0=ALU.mult), reads=["rr", "nlam"], writes=["rr"])
                    for il in range(nt):
                        P.add("dve", mk("tensor_scalar", out=odt[:, il, :], in0=Osb[:, il, 0:128],
                                        scalar1=rr[:, il:il + 1], scalar2=None, op0=ALU.mult),
                              reads=osr + ["rr"], writes=[("odt", il)])
                        P.add("dve", mk("scalar_tensor_tensor", out=odt[:, il, :], in0=Osb[:, nt + il, 0:128],
                                        scalar=rr[:, nt + il:nt + il + 1], in1=odt[:, il, :], op0=ALU.mult,
                                        op1=ALU.add),
                              reads=osr + ["rr", ("odt", il)], writes=[("odt", il)])

                def ev2(h=h):
                    for il in range(nt):
                        P.add("dve", mk("scalar_tensor_tensor", out=junkd[:], in0=odt[:, il, :], scalar=1.0,
                                        in1=odt[:, il, :], op0=ALU.mult, op1=ALU.mult,
                                        accum_out=st_s[:, 8 + il:9 + il]),
                              reads=[("odt", il)], writes=["junkd", ("ssD", il)])

                def ev3(h=h):
                    rstd_ops(st_s[:, 8:8 + nt], st_s[:, 52:52 + nt], 1.0 / 128, [("ssD", il) for il in range(nt)],
                             ["rsD"])

                def ev4(h=h):
                    for il in range(nt):
                        P.add("dve", mk("scalar_tensor_tensor", out=y_tok[:, il, h * 128:(h + 1) * 128],
                                        in0=odt[:, il, :], scalar=st_s[:, 52 + il:53 + il], in1=subln8[:],
                                        op0=ALU.mult, op1=ALU.mult),
                              reads=[("odt", il), "rsD", "subln8"], writes=[("ytok", il, h)])

                if h < 3:
                    for kk_, fn_ in ((2, ev1), (5, ev2), (8, ev3), (12, ev4)):
                        hg_sched.setdefault((h + 1, kk_), []).append(fn_)
                else:
                    ev1(); ev2(); ev3(); ev4()

            stage(f'g{gi}C')
            stage(f'g{gi}D')
            P.phase = f'g{gi}.E'
            yT = actT
            s_wo = [use((gi, "wo", 0)), slot((gi, "wo", 1))]
            wo = [v8(ws[s_wo[0]]), v8(ws[s_wo[1]])]
            gpost3 = C("gpost").rearrange("p (a b) -> p a b", a=2)

            def e_s0(il):
                ti = t0 + il
                sl = slice(il * 128, (il + 1) * 128)
                ytr = [("ytok", il, hh) for hh in range(5)]
                if debug:
                    P.add("sp", mk("dma_start", out=dbg_y[ti * 128:(ti + 1) * 128, :], in_=y_tok[:, il, :]),
                          reads=ytr, writes=[("dbgy", ti)], dma=f"dy{il}")
                transpose8(y_tok[:, il, :], ytr, yT[:, 0:8, sl],
                           [("actT", il)] + [("actT", "c", ci) for ci in range(8)], None)

            def e_s1(il):
                sl = slice(il * 128, (il + 1) * 128)
                b0 = 4 + 2 * (il % 2)
                for kc in range(8):
                    for dh in range(2):
                        P.add("pe", mk("matmul", ps[:, b0 + dh, :], lhsT=yT[:, kc, sl], rhs=wo[dh][:, kc, :],
                                       start=(kc == 0), stop=(kc == 7)),
                              reads=[("actT", il), ("ws", s_wo[dh])], writes=[("ps", b0 + dh)])
                mixv = ps[:, b0:b0 + 2, :]
                P.add("act", mk("activation", out=junk[:].rearrange("p (a b) -> p a b", a=2), in_=mixv,
                                func=AF.Square, accum_out=st_s[:, 20 + il:21 + il]),
                      reads=[("ps", b0), ("ps", b0 + 1)], writes=["junk", ("ssM", il)])
                rstd_ops(st_s[:, 20 + il:21 + il], st_s[:, 24 + il:25 + il], 1.0 / D, [("ssM", il)], [("rsM", il)])

            def e_s2(il):
                ti = t0 + il
                b0 = 4 + 2 * (il % 2)
                mixv = ps[:, b0:b0 + 2, :]
                tmpM = fwA[il % 2][:, 0:2, :]
                tr = [("fwA", il % 2, 0), ("fwA", il % 2, 1)]
                P.add("dve", mk("scalar_tensor_tensor", out=tmpM, in0=mixv, scalar=st_s[:, 24 + il:25 + il],
                                in1=gpost3, op0=ALU.mult, op1=ALU.mult),
                      reads=[("ps", b0), ("ps", b0 + 1), ("rsM", il), "cst"], writes=tr)
                P.add("dve", mk("tensor_tensor", out=h2[:, il, :], in0=h2[:, il, :],
                                in1=tmpM.rearrange("p a b -> p (a b)"), op=ALU.add),
                      reads=[("h2", il)] + tr, writes=[("h2", il)])
                if debug:
                    P.add("sp", mk("dma_start", out=dbg_h2[ti * 128:(ti + 1) * 128, :], in_=h2[:, il, :]),
                          reads=[("h2", il)], writes=[("dbgh", ti)], dma=f"dh{il}")
                P.add("act", mk("activation", out=junk[:], in_=h2[:, il, :], func=AF.Square,
                                accum_out=st_s[:, 28 + il:29 + il]),
                      reads=[("h2", il)], writes=["junk", ("ss2", il)])
                rstd_ops(st_s[:, 28 + il:29 + il], st_s[:, 32 + il:33 + il], 1.0 / D, [("ss2", il)], [("rs2", il)])
                at = atok[il % 2]
                P.add("act", mk("activation", out=at[:], in_=h2[:, il, :], func=AF.Copy, scale=st_s[:, 32 + il:33 + il]),
                      reads=[("h2", il), ("rs2", il)], writes=[("atok", il % 2)])

            def e_s3(il):
                sl = slice(il * 128, (il + 1) * 128)
                transpose8(atok[il % 2], [("atok", il % 2)], aT[:, :, sl], [("aT", il)], C("gfprec"))

            for step in range(nt + 3):
                if step < nt:
                    e_s0(step)
                if 0 <= step - 1 < nt:
                    e_s1(step - 1)
                if 0 <= step - 2 < nt:
                    e_s2(step - 2)
                if 0 <= step - 3 < nt:
                    e_s3(step - 3)

            stage(f'g{gi}E')
            P.phase = f'g{gi}.F'
            def vg(t):
                return t[:, 0:4096].rearrange("p (k c) -> p k c", k=8)[:, :, 0:256]

            def vu(t):
                return t[:, 0:4096].rearrange("p (k c) -> p k c", k=8)[:, :, 256:512]

            def ffn_x(ci, s):
                cc = ci % 2
                wgu = (vg(ws[s]), vu(ws[s]))
                fs = ci % 2
                A, U = fwA[fs], fwU[fs]
                if gi == 1:
                    hbk = 4 + (ci % 4)
                    for half in range(2):
                        for kc in range(8):
                            P.add("pe", mk("matmul", ps[:, hbk, 2 * half:2 * half + 2],
                                           lhsT=wgu[half][:, kc, cc * 128:(cc + 1) * 128], rhs=a2h[:, kc, :],
                                           start=(kc == 0), stop=(kc == 7), skip_group_check=True),
                                  reads=[("ws", s), "a2h"], writes=[("ps", hbk)])
                for half in range(2):
                    b = gbank()
                    cidx = ci + NCH * half
                    yv = A[:, half, 0:N]
                    yr = ("fwA", fs, half)
                    ur = ("fwU", fs, half)
                    for kc in range(8):
                        P.add("pe", mk("matmul", ps[:, b, 0:N], lhsT=wgu[half][:, kc, cc * 128:(cc + 1) * 128],
                                       rhs=aT[:, kc, 0:N], start=(kc == 0), stop=(kc == 7)),
                              reads=[("ws", s)] + aT_res, writes=[("ps", b)])
                    P.add("act", mk("copy", out=U[:, half, 2:2 + N], in_=ps[:, b, 0:N]),
                          reads=[("ps", b)], writes=[ur])
                    P.add("act", mk("activation", out=yv, in_=ps[:, b, 0:N], func=AF.Identity,
                                    scale=cw[:, cidx, 2:3], bias=cb[:, cidx:cidx + 1]),
                          reads=[("ps", b), "cst"], writes=[yr])

            def ffn_t(ci):
                fs = ci % 2
                A, U = fwA[fs], fwU[fs]
                hview = halo[:].rearrange("p (h c) r -> p h c r", h=2)[:, :, ci, :]
                urs = [("fwU", fs, 0), ("fwU", fs, 1)]
                if gi == 1:
                    hbk = 4 + (ci % 4)
                    hsrc = ps[:, hbk, 0:4].rearrange("p (h r) -> p h r", h=2)
                    P.add("dve", mk("tensor_copy", out=U[:, :, 0:2], in_=hsrc), reads=[("ps", hbk)], writes=urs)
                else:
                    P.add("dve", mk("tensor_copy", out=U[:, :, 0:2], in_=hview),
                          reads=["halo_all", ("halo", ci)], writes=urs)
                P.add("dve", mk("tensor_copy", out=hview, in_=U[:, :, N:N + 2]),
                      reads=urs, writes=[("halo", ci)])
                for half in range(2):
                    cidx = ci + NCH * half
                    yv = A[:, half, 0:N]
                    yr = ("fwA", fs, half)
                    ur = ("fwU", fs, half)
                    for r in (1, 0):
                        P.add("dve", mk("scalar_tensor_tensor", out=yv, in0=U[:, half, r:r + N],
                                        scalar=cw[:, cidx, r:r + 1], in1=yv, op0=ALU.mult, op1=ALU.add),
                              reads=[ur, yr, "cst"], writes=[yr])

            def ffn_z(ci):
                fs = ci % 2
                A = fwA[fs]
                yg, yu, sq = A[:, 0, 0:N], A[:, 1, 0:N], A[:, 2, 0:N]
                rg, ru, rq = ("fwA", fs, 0), ("fwA", fs, 1), ("fwA", fs, 2)
                P.add("act", mk("activation", out=sq, in_=yg, func=AF.Gelu_apprx_tanh), reads=[rg], writes=[rq])
                P.add("dve", mk("tensor_tensor", out=actT[:, ci, 0:N], in0=sq, in1=yu, op=ALU.mult),
                      reads=[rq, ru],
                      writes=[("actT", "c", ci)] + ([("actT", il) for il in range(nt)] if ci < 8 else []))

            s_up = None
            if gi == 0:
                P.add("dve", mk("tensor_copy", out=a2h[:], in_=aT[:, :, N - 2:N]), reads=aT_res, writes=["a2h"])
            else:
                for ci in range(NCH + 1):
                    if ci < NCH:
                        if ci % 2 == 0:
                            s_up = use((gi, "up", ci // 2))
                        ffn_x(ci, s_up)
                        ffn_t(ci)
                    if ci >= 1:
                        ffn_z(ci - 1)
            stage(f'g{gi}F')
            if gi == 0:
                preloadA(1)
                phaseA(1)
                reload_h2(1)
                continue
            if gi + 1 < len(GROUPS):
                preloadA(gi + 1)
            P.phase = f'g{gi}.G'
            for half in range(2):
                for q in range(6):
                    c0 = 4 * q
                    c1 = min(c0 + 4, NCH)
                    ncq = c1 - c0
                    s_dn = use((gi, "dn", half, q))
                    wd = ws[s_dn][:, 0:ncq * 1024].rearrange("p (c d) -> p c d", c=ncq)
                    for il in (2 * half, 2 * half + 1):
                        sl = slice(il * 128, (il + 1) * 128)
                        for ci in range(c0, c1):
                            for dh in range(2):
                                bank = dh * 4 + il
                                P.add("pe", mk("matmul", ps[:, bank, :], lhsT=actT[:, ci, sl],
                                               rhs=wd[:, ci - c0, dh * 512:(dh + 1) * 512],
                                               start=(ci == 0), stop=(ci == NCH - 1)),
                                      reads=[("actT", "c", ci), ("ws", s_dn)], writes=[("ps", bank)])

            def final_norm(tiles):
                P.phase = f'g{gi}.H'
                k0 = tiles[0]
                n = len(tiles)
                for il in tiles:
                    for dh in range(2):
                        P.add("act", mk("activation", out=junk[:, 0:512], in_=ps[:, dh * 4 + il, :], func=AF.Square,
                                        accum_out=st_s[:, 36 + 2 * il + dh:37 + 2 * il + dh]),
                              reads=[("ps", dh * 4 + il)], writes=["junk", ("ssF", il, dh)])
                ssv = st_s[:, 36 + 2 * k0:36 + 2 * (k0 + n)].rearrange("p (a b) -> p a b", b=2)
                P.add("dve", mk("tensor_tensor", out=st_s[:, 44 + k0:44 + k0 + n], in0=ssv[:, :, 0], in1=ssv[:, :, 1],
                                op=ALU.add),
                      reads=[("ssF", il, dh) for il in tiles for dh in range(2)], writes=[("ssFt", k0)])
                rstd_ops(st_s[:, 44 + k0:44 + k0 + n], st_s[:, 48 + k0:48 + k0 + n], 1.0 / D, [("ssFt", k0)],
                         [("rsF", k0)])
                tmps = [(gsb[:], "gsb"), (tmpH[:], "tmpH")]
                for il in tiles:
                    ti = t0 + il
                    for dh in range(2):
                        tb, tres = tmps[dh]
                        hs = slice(dh * 512, (dh + 1) * 512)
                        P.add("dve", mk("scalar_tensor_tensor", out=tb, in0=ps[:, dh * 4 + il, :],
                                        scalar=st_s[:, 48 + il:49 + il], in1=gfpost[:, hs], op0=ALU.mult, op1=ALU.mult),
                              reads=[("ps", dh * 4 + il), ("rsF", k0), "cst"], writes=[tres])
                        P.add("dve", mk("tensor_tensor", out=h2[:, il, hs], in0=h2[:, il, hs], in1=tb, op=ALU.add),
                              reads=[("h2", il), tres], writes=[("h2", il)])
                    P.add("sp", mk("dma_start", out=out[(ti - 1) * 128:ti * 128, :], in_=h2[:, il, :]),
                          reads=[("h2", il)], writes=[("out", ti)], dma=f"o{il}")

            final_norm([0, 1])
            if gi + 1 < len(GROUPS):
                ring_allowed[0] = [0, 1]
                phaseA(gi + 1)
                ring_allowed[0] = [0, 1, 2, 3]
            final_norm([2, 3])
            if gi + 1 < len(GROUPS):
                reload_h2(gi + 1)

        fin = [("out", ti) for ti in range(1, NT)]
        if debug:
            fin += [("dbgy", ti) for ti in range(NT)] + [("dbgh", ti) for ti in range(NT)]
        P.add("sp", lambda e: None, reads=fin, force=True)

        semd = {}

        def sem_ctx(name):
            if name not in semd:
                semd[name] = es.enter_context(nc.semaphore(name))
            return semd[name]

        P.finalize(sem_ctx)
        with nc.Block() as block:
            @block.tensor
            def _(e):
                P.run("pe", e)

            @block.scalar
            def _(e):
                P.run("act", e)

            @block.vector
            def _(e):
                P.run("dve", e)

            @block.gpsimd
            def _(e):
                P.run("pool", e)

            @block.sync
            def _(e):
                P.run("sp", e)
    return nc


def _rel_bucket_static(dist):
    n = np.maximum(dist, 0)
    nf = np.maximum(n, 1).astype(np.float32)
    large = 16 + (np.log(nf / np.float32(16)) / np.float32(math.log(128 / 16)) * np.float32(16)).astype(np.int32)
    large = np.minimum(large, 31)
    return np.where(n < 16, n, large)


def _unit_major(w, col_blocks):
    w = np.asarray(w, np.float32)
    out = np.empty((len(col_blocks) * 128, 4096), np.float32)
    for u, cols in enumerate(col_blocks):
        blk = w[:, cols].reshape(8, 128, 512).transpose(1, 0, 2).reshape(128, 4096)
        out[u * 128:(u + 1) * 128] = blk
    return out


def _w_up_layout(w_up):
    idx = np.concatenate([np.concatenate([np.arange(256 * ju, 256 * ju + 256),
                                          DFF + np.arange(256 * ju, 256 * ju + 256)]) for ju in range(11)])
    return _unit_major(w_up, [idx[512 * ju:512 * ju + 512] for ju in range(11)])


def _consts(inp):
    c = np.zeros((128, NCONST), np.float32)

    def put(name, arr):
        o, w = _C[name]
        c[:, o:o + w] = np.asarray(arr, np.float32).reshape(128, w)

    bc = lambda v: np.broadcast_to(np.asarray(v, np.float32)[None, :], (128, len(v)))
    put("gpost", bc(inp["ln_mix_post"][0]))
    put("gfpost", bc(inp["ln_ffn_post"][0]))
    put("gprec", inp["ln_mix_pre"][0].reshape(8, 128).T)
    put("gfprec", inp["ln_ffn_pre"][0].reshape(8, 128).T)
    cwv = inp["ffn_conv_w"][0]
    put("cw", cwv.reshape(3, 44, 128).transpose(2, 1, 0))
    put("cb", inp["ffn_conv_b"][0].reshape(44, 128).T)
    put("lbl", inp["hgrn_lb_logits"].reshape(2, 4, 128).transpose(2, 0, 1))
    put("lamv", bc(inp["diff_lambda"][0].reshape(-1)))
    put("subln", bc(inp["diff_subln"][0]))
    put("gnorm", bc(inp["hgrn_gnorm"][0]))
    tbl = np.asarray(inp["rel_bias_table"], np.float32)
    kl = np.arange(128)[:, None]
    ql = np.arange(128)[None, :]
    bd = np.zeros((128, 4, 2, 128), np.float32)
    for kind in range(2):
        dist = 128 * kind + ql - kl
        bidx = _rel_bucket_static(dist)
        for h in range(4):
            g = tbl[bidx, h]
            bd[:, h, kind, :] = np.where(dist >= 0, g, np.float32(NEG))
    put("biasd", bd)
    put("t31", bc(tbl[31, :]))
    put("hmask", (kl <= ql).astype(np.float32))
    vv = np.ones((128, 1), np.float32)
    vv[:NPAD] = 0.0
    put("vvalid", vv)
    put("ident", np.eye(128, dtype=np.float32))
    return c


def kernel(**inp):
    x = np.asarray(inp["x"], np.float32)
    B = x.shape[0]
    consts = _consts(inp)
    meta = np.asarray(inp["meta_tokens"], np.float32)
    shared = {
        "w_in": _unit_major(inp["w_in"][0], [np.arange(512 * u, 512 * u + 512) for u in range(7)]),
        "w_out": _unit_major(inp["w_out"][0], [np.arange(512 * u, 512 * u + 512) for u in range(2)]),
        "w_up": _w_up_layout(inp["w_ffn_up"][0]),
        "w_dn": np.ascontiguousarray(inp["w_ffn_down"][0], np.float32),
        "consts": consts,
    }
    in_maps = []
    for b in range(B):
        xp = np.zeros((LP, D), np.float32)
        xp[NPAD:128] = meta
        xp[128:] = x[b]
        m = dict(shared)
        m["xp"] = xp
        in_maps.append(m)
    nc = build_nc()
    res = run_bass_kernel_spmd(nc, in_maps, core_ids=list(range(B)))
    return np.stack([np.asarray(r["out"], np.float32).reshape(2048, D) for r in res.results], axis=0)
```
